# Optimizing a Trainium2 kernel written in Bass

```python
import jax, jax.numpy as jnp
from jax import lax
import numpy as np

D_MODEL = 1024
BATCH = 8
SEQ = 4096
DEPTH = 1

PLE_DIM = 256
MIX_WIDTH = D_MODEL
HG_WIDTH = D_MODEL // 2
HG_HEADS = 4
HG_DK = HG_WIDTH // HG_HEADS
HG_DV = HG_WIDTH // HG_HEADS
HG_CHUNK = 64
SB_WIDTH = MIX_WIDTH - HG_WIDTH
SB_HEADS = 8
SB_DH = SB_WIDTH // SB_HEADS
SB_BLOCK = 128
D_FF = -(-8 * D_MODEL // (3 * 256)) * 256
IN_COLS = 4 * HG_WIDTH + 3 * SB_WIDTH
EPS = 1e-6

kernel_name = "hybrid_hgrn2_stickbreaking_block"


def rmsnorm(x, w):
    xf = x.astype(jnp.float32)
    xf = xf * lax.rsqrt(jnp.mean(xf * xf, axis=-1, keepdims=True) + EPS)
    return xf.astype(x.dtype) * w


def _to_chunks(a, heads, d):
    B, T, _ = a.shape
    return a.reshape(B, T // HG_CHUNK, HG_CHUNK, heads, d).transpose(1, 0, 3, 2, 4)


def hgrn2_mix(q, f_logit, i_in, lb):
    B, T, _ = q.shape
    q = jax.nn.silu(q.astype(jnp.float32))
    z = f_logit.astype(jnp.float32)
    log_f = jnp.logaddexp(jnp.log(lb), jnp.log1p(-lb) + jax.nn.log_sigmoid(z))
    k = -jnp.expm1(log_f)
    v = i_in.astype(jnp.float32)
    qc = _to_chunks(q, HG_HEADS, HG_DK)
    kc = _to_chunks(k, HG_HEADS, HG_DK)
    gc = _to_chunks(log_f, HG_HEADS, HG_DK)
    vc = _to_chunks(v, HG_HEADS, HG_DV)
    incl = jnp.tril(jnp.ones((HG_CHUNK, HG_CHUNK), dtype=bool))

    def step(S, inp):
        qb, kb, gb, vb = inp
        b = jnp.cumsum(gb, axis=2)
        o_inter = jnp.einsum('bhtk,bhkv->bhtv', qb * jnp.exp(b), S)
        diff = b[:, :, :, None, :] - b[:, :, None, :, :]
        decay = jnp.where(incl[:, :, None], jnp.exp(jnp.minimum(diff, 0.0)), 0.0)
        scores = jnp.einsum('bhtsk,bhsk->bhts', qb[:, :, :, None, :] * decay, kb)
        o_intra = jnp.einsum('bhts,bhsv->bhtv', scores, vb)
        b_last = b[:, :, -1:, :]
        S_new = jnp.exp(b_last[:, :, 0, :])[..., None] * S + jnp.einsum(
            'bhsk,bhsv->bhkv', kb * jnp.exp(b_last - b), vb)
        return S_new, o_inter + o_intra

    S0 = jnp.zeros((B, HG_HEADS, HG_DK, HG_DV), jnp.float32)
    _, oc = lax.scan(step, S0, (qc, kc, gc, vc))
    return oc.transpose(1, 0, 3, 2, 4).reshape(B, T, HG_HEADS * HG_DV)


def stick_breaking_mix(q, k, v):
    B, T, H, d = q.shape
    scale = d ** -0.5
    outs = []
    for blk in range(T // SB_BLOCK):
        t0 = blk * SB_BLOCK
        t1 = t0 + SB_BLOCK
        qb = q[:, t0:t1]
        kp = k[:, :t1]
        vp = v[:, :t1]
        z = jnp.einsum('bthd,bshd->bhts', qb, kp).astype(jnp.float32) * scale
        causal = jnp.arange(t1)[None, :] < jnp.arange(t0, t1)[:, None]
        log_1mb = jnp.where(causal, -jax.nn.softplus(z), 0.0)
        rem = lax.cumsum(log_1mb, axis=3, reverse=True) - log_1mb
        a = jnp.where(causal, jnp.exp(jax.nn.log_sigmoid(z) + rem), 0.0)
        outs.append(jnp.einsum('bhts,bshd->bthd', a.astype(v.dtype), vp))
    return jnp.concatenate(outs, axis=1)


def setup_inputs(seed: int = 0) -> dict:
    key = jax.random.key(seed)
    ks = jax.random.split(key, 16)
    f32 = jnp.float32
    nrm = lambda k, shape, s: jax.random.normal(k, shape, f32) * s
    gain = lambda k, shape: 1.0 + 0.05 * jax.random.normal(k, shape, f32)
    return {
        "x": jax.random.normal(ks[0], (BATCH, SEQ, D_MODEL), f32),
        "p": jax.random.normal(ks[1], (DEPTH, BATCH, SEQ, PLE_DIM), f32),
        "attn_pre_norm": gain(ks[2], (DEPTH, D_MODEL)),
        "w_in": nrm(ks[3], (DEPTH, D_MODEL, IN_COLS), D_MODEL ** -0.5),
        "hg_lower_gamma": nrm(ks[4], (DEPTH + 1, HG_WIDTH), 0.5),
        "hg_out_norm": gain(ks[5], (DEPTH, HG_WIDTH)),
        "sb_out_norm": gain(ks[6], (DEPTH, SB_WIDTH)),
        "w_out": nrm(ks[7], (DEPTH, MIX_WIDTH, D_MODEL), MIX_WIDTH ** -0.5),
        "attn_post_norm": gain(ks[8], (DEPTH, D_MODEL)),
        "ffn_pre_norm": gain(ks[9], (DEPTH, D_MODEL)),
        "w_gate_up": nrm(ks[10], (DEPTH, D_MODEL, 2 * D_FF), D_MODEL ** -0.5),
        "w_down": nrm(ks[11], (DEPTH, D_FF, D_MODEL), D_FF ** -0.5),
        "ffn_post_norm": gain(ks[12], (DEPTH, D_MODEL)),
        "ple_proj": nrm(ks[13], (DEPTH, PLE_DIM, D_MODEL), PLE_DIM ** -0.5),
        "ple_gate": nrm(ks[14], (DEPTH, D_MODEL, D_MODEL), D_MODEL ** -0.5),
    }


def reference(x, p, attn_pre_norm, w_in, hg_lower_gamma, hg_out_norm, sb_out_norm, w_out,
              attn_post_norm, ffn_pre_norm, w_gate_up, w_down, ffn_post_norm, ple_proj, ple_gate):
    B, T, _ = x.shape
    lb_all = jnp.cumsum(jax.nn.softmax(hg_lower_gamma.astype(jnp.float32), axis=0), axis=0)
    splits = [HG_WIDTH, 2 * HG_WIDTH, 3 * HG_WIDTH, 4 * HG_WIDTH,
              4 * HG_WIDTH + SB_WIDTH, 4 * HG_WIDTH + 2 * SB_WIDTH]
    h = x
    for i in range(DEPTH):
        u = rmsnorm(h, attn_pre_norm[i])
        proj = u @ w_in[i]
        hq, hf, hi, hg, sq, sk, sv = jnp.split(proj, splits, axis=-1)
        o_hg = hgrn2_mix(hq, hf, hi, lb_all[i])
        o_hg = rmsnorm(o_hg.astype(x.dtype), hg_out_norm[i]) * jax.nn.silu(hg)
        o_sb = stick_breaking_mix(sq.reshape(B, T, SB_HEADS, SB_DH),
                                  sk.reshape(B, T, SB_HEADS, SB_DH),
                                  sv.reshape(B, T, SB_HEADS, SB_DH)).reshape(B, T, SB_WIDTH)
        o_sb = rmsnorm(o_sb, sb_out_norm[i])
        mix = jnp.concatenate([o_hg, o_sb], axis=-1) @ w_out[i]
        h = h + rmsnorm(mix, attn_post_norm[i])
        u = rmsnorm(h, ffn_pre_norm[i])
        gate, up = jnp.split(u @ w_gate_up[i], [D_FF], axis=-1)
        y = (jax.nn.silu(gate) * up) @ w_down[i]
        h = h + rmsnorm(y, ffn_post_norm[i])
        h = h + (p[i] @ ple_proj[i]) * jax.nn.sigmoid(h @ ple_gate[i])
    return h
```

```python
import contextlib
import os as _os
_DBGH = int(_os.environ.get('DBGH', '0'))
import numpy as np
import concourse.bass as bass
import concourse.mybir as mybir
from concourse.bass_utils import run_bass_kernel_spmd

F32 = mybir.dt.float32
BF16 = mybir.dt.bfloat16
AF = mybir.ActivationFunctionType
ALU = mybir.AluOpType
AX = mybir.AxisListType

D = 1024
T = 4096
PLE = 256
HGW = 512
SBW = 512
DFF = 2816
INC = 3584
EPS = 1e-6
TT = 512
NTT = T // TT
P = 128
NCORES = 8

ENGS = ("pe", "act", "dve", "pool", "sp")


class Res:
    __slots__ = ("name", "w", "r", "alias", "excl")

    def __init__(self, name, excl=False):
        self.name = name
        self.w = None
        self.r = {}
        self.alias = []
        self.excl = excl


class Ev:
    __slots__ = ("kind", "eng", "idx", "vc", "op")

    def __init__(self, kind, eng, idx, vc, op=None):
        self.kind = kind
        self.eng = eng
        self.idx = idx
        self.vc = vc
        self.op = op


class Op:
    __slots__ = ("eng", "idx", "fn", "waits", "signal", "dma", "name")

    def __init__(self, eng, idx, fn, name=""):
        self.eng = eng
        self.idx = idx
        self.fn = fn
        self.waits = []
        self.signal = False
        self.dma = None
        self.name = name


class Prog:
    def __init__(self, nc, stack, n_dma_sems=60, n_pool_sems=12):
        self.n_pool_sems = n_pool_sems
        self.next_dsem_q = {}
        self.nc = nc
        self.stack = stack
        self.ops = {e: [] for e in ENGS}
        self.vc = {e: {f: 0 for f in ENGS} for e in ENGS}
        self.seen_d = {e: {} for e in ENGS}
        self.sem = {e: stack.enter_context(nc.semaphore("s_" + e)) for e in ENGS}
        self.dsem = [stack.enter_context(nc.semaphore("d%d" % i)) for i in range(n_dma_sems)]
        self.dcnt = [0] * n_dma_sems
        self.res_dsem = {}
        self.next_dsem = 0
        self.nres = 0

    def res(self, name=None, excl=False):
        self.nres += 1
        return Res(name or ("r%d" % self.nres), excl)

    def alias(self, a, b):
        a.alias.append(b)
        b.alias.append(a)

    def _need(self, op, ev):
        e = op.eng
        if ev is None:
            return
        if ev.kind == "c":
            if ev.eng == e and e in ("pe", "sp"):
                return
            if self.vc[e][ev.eng] >= ev.idx + 1:
                return
            op.waits.append(ev)
            ev.op.signal = True
            self.vc[e][ev.eng] = ev.idx + 1
            for f, v in ev.vc.items():
                if self.vc[e][f] < v:
                    self.vc[e][f] = v
        else:
            slot, val = ev.eng, ev.idx
            if self.seen_d[e].get(slot, 0) >= val:
                return
            op.waits.append(ev)
            self.seen_d[e][slot] = val
            for f, v in ev.vc.items():
                if self.vc[e][f] < v:
                    self.vc[e][f] = v

    def _expand(self, lst):
        out = []
        for r in lst:
            out.append(r)
            out.extend(r.alias)
        return out

    def _deps(self, op, reads, writes):
        for r in reads:
            self._need(op, r.w)
        for r in writes:
            self._need(op, r.w)
            for ev in r.r.values():
                self._need(op, ev)

    def op(self, eng, fn, reads=(), writes=(), name=""):
        reads = self._expand(reads)
        writes = self._expand(writes)
        ex = [r for r in reads if r.excl]
        if ex:
            writes = writes + [r for r in ex if r not in writes]
            reads = [r for r in reads if not r.excl]
        lst = self.ops[eng]
        o = Op(eng, len(lst), fn, name)
        lst.append(o)
        self._deps(o, reads, writes)
        ev = Ev("c", eng, o.idx, dict(self.vc[eng]), o)
        for r in writes:
            r.w = ev
            r.r = {}
        for r in reads:
            r.r[eng] = ev
        return o

    def dma(self, q, out, in_, reads=(), writes=(), name="", **kw):
        reads = self._expand(reads)
        writes = self._expand(writes)
        lst = self.ops[q]
        o = Op(q, len(lst), None, name)
        lst.append(o)
        self._deps(o, reads, writes)
        key = (q, writes[0] if writes else reads[0])
        if key not in self.res_dsem:
            lo, n = (0, self.n_pool_sems) if q == "pool" else (self.n_pool_sems, len(self.dsem) - self.n_pool_sems)
            self.res_dsem[key] = lo + self.next_dsem_q.get(q != "pool", 0) % n
            self.next_dsem_q[q != "pool"] = self.next_dsem_q.get(q != "pool", 0) + 1
        slot = self.res_dsem[key]
        prev = self.dcnt[slot]
        if prev and self.seen_d[q].get(slot, 0) < prev:
            o.waits.append(Ev("d", slot, prev, {}))
            self.seen_d[q][slot] = prev
        self.dcnt[slot] += 16
        val = self.dcnt[slot]
        o.dma = (slot, out, in_, kw)
        ev = Ev("d", slot, val, dict(self.vc[q]), o)
        for r in writes:
            r.w = ev
            r.r = {}
        for r in reads:
            r.r[("d", slot)] = ev
        return o

    def finish(self, eng, all_res):
        o = Op(eng, len(self.ops[eng]), "nop", "finish")
        self.ops[eng].append(o)
        for r in all_res:
            self._need(o, r.w)
            for ev in r.r.values():
                self._need(o, ev)
        return o

    def emit(self):
        nc = self.nc
        sigcount = {}
        for e in ENGS:
            c = 0
            lst = []
            for o in self.ops[e]:
                if o.signal:
                    c += 1
                lst.append(c)
            sigcount[e] = lst
        engobj = {"pe": "tensor", "act": "scalar", "dve": "vector", "pool": "gpsimd", "sp": "sync"}

        def run(e, eng):
            for o in self.ops[e]:
                for ev in o.waits:
                    if ev.kind == "c":
                        eng.wait_ge(self.sem[ev.eng], sigcount[ev.eng][ev.idx])
                    else:
                        eng.wait_ge(self.dsem[ev.eng], ev.idx)
                if o.dma is not None:
                    slot, out, in_, kw = o.dma
                    eng.dma_start(out=out, in_=in_, **kw).then_inc(self.dsem[slot], 16)
                elif o.fn == "nop":
                    pass
                else:
                    ins = o.fn(eng)
                    if o.signal:
                        ins.then_inc(self.sem[e], 1)

        with nc.Block() as block:
            @block.tensor
            def _(eng):
                run("pe", eng)

            @block.scalar
            def _(eng):
                run("act", eng)

            @block.vector
            def _(eng):
                run("dve", eng)

            @block.gpsimd
            def _(eng):
                run("pool", eng)

            @block.sync
            def _(eng):
                run("sp", eng)


def build(T=T, taps=None):
    NT = T // TT
    taps = taps or ()
    nc = bass.Bass("TRN2", target_bir_lowering=False)
    dr = lambda name, shape, dt=F32, kind="ExternalInput": nc.dram_tensor(name, shape, dt, kind=kind).ap()
    x = dr("x", [T, D])
    p_in = dr("p", [T, PLE])
    attn_pre = dr("attn_pre_norm", [D])
    w_in = dr("w_in", [D, INC])
    hg_gamma = dr("hg_lower_gamma", [2, HGW])
    hg_norm = dr("hg_out_norm", [HGW])
    sb_norm = dr("sb_out_norm", [SBW])
    w_out = dr("w_out", [D, D])
    attn_post = dr("attn_post_norm", [D])
    ffn_pre = dr("ffn_pre_norm", [D])
    w_gu = dr("w_gate_up", [D, 2 * DFF])
    w_down = dr("w_down", [DFF, D])
    ffn_post = dr("ffn_post_norm", [D])
    ple_proj = dr("ple_proj", [PLE, D])
    ple_gate = dr("ple_gate", [D, D])
    y = dr("y", [T, D], F32, "ExternalOutput")
    w_in_b = dr("w_in_b", [D, INC], BF16, "Internal")
    w_out_b = dr("w_out_b", [D, D], BF16, "Internal")
    w_gu_b = dr("w_gu_b", [D, 2 * DFF], BF16, "Internal")
    w_down_b = dr("w_down_b", [DFF, D], BF16, "Internal")
    ple_proj_b = dr("ple_proj_b", [PLE, D], BF16, "Internal")
    ple_gate_b = dr("ple_gate_b", [D, D], BF16, "Internal")

    with contextlib.ExitStack() as st:
        pg = Prog(nc, st)
        sb = lambda name, shape, dt: st.enter_context(nc.sbuf_tensor(name, shape, dt))
        ps = lambda name, shape, dt: st.enter_context(nc.psum_tensor(name, shape, dt))
        R = pg.res

        def ACT(func, out, in_, reads, writes, **kw):
            pg.op("act", lambda e: e.activation(out=out, in_=in_, func=func, **kw), reads=reads, writes=writes)

        def MM(out, lhsT, rhs, start, stop, reads, writes, skip=False):
            pg.op("pe", lambda e: e.matmul(out, lhsT=lhsT, rhs=rhs, start=start, stop=stop, skip_group_check=skip),
                  reads=reads, writes=writes)

        def TT_(eng, out, in0, in1, op, reads, writes):
            pg.op(eng, lambda e: e.tensor_tensor(out=out, in0=in0, in1=in1, op=op), reads=reads, writes=writes)

        def TS(eng, out, in0, s1, s2, op0, op1, reads, writes):
            if s2 is None:
                pg.op(eng, lambda e: e.tensor_scalar(out=out, in0=in0, scalar1=s1, scalar2=None, op0=op0), reads=reads, writes=writes)
            else:
                pg.op(eng, lambda e: e.tensor_scalar(out=out, in0=in0, scalar1=s1, scalar2=s2, op0=op0, op1=op1), reads=reads, writes=writes)

        def STT(eng, out, in0, scalar, in1, op0, op1, reads, writes):
            pg.op(eng, lambda e: e.scalar_tensor_tensor(out=out, in0=in0, scalar=scalar, in1=in1, op0=op0, op1=op1),
                  reads=reads, writes=writes)

        def CP(eng, out, in_, reads, writes):
            if eng == "act":
                pg.op("act", lambda e: e.copy(out=out, in_=in_), reads=reads, writes=writes)
            else:
                pg.op(eng, lambda e: e.tensor_copy(out=out, in_=in_), reads=reads, writes=writes)

        def MEMSET(eng, ap, val, writes, reads=()):
            pg.op(eng, lambda e: e.memset(ap, val), reads=reads, writes=writes)

        tap_outs = []

        def TAP(name, ap, reads, ti=0, only_ti=0):
            if name not in taps or ti != only_ti:
                return
            shape = list(ap.shape)
            d = nc.dram_tensor("tap_" + name, shape, ap.dtype, kind="ExternalOutput").ap()
            rr = R()
            pg.dma("sp", d, ap, reads=reads, writes=[rr])
            tap_outs.append(rr)

        psall = ps("psall", [P, 8 * 512], F32)
        bk = [psall[:, i * 512:(i + 1) * 512] for i in range(8)]
        rb = [R("bk%d" % i, excl=True) for i in range(8)]
        bkT = psall[:, 0:512].bitcast(BF16)
        r_bkT = rb[0]

        ident = sb("ident", [P, P], BF16); r_ident = R()
        ones_bf = sb("ones_bf", [P, P], BF16); r_ones = R()
        zeros_bf = sb("zeros_bf", [P, P], BF16); r_zeros = R()
        tmpf = sb("tmpf", [P, P], F32); r_tmpf = R()
        tri_bf = sb("tri_bf", [P, P], BF16); r_tri = R()
        su_bf = sb("su_bf", [P, P], BF16); r_su = R()
        cmask = sb("cmask", [P, P], BF16); r_cmask = R()
        mh = sb("mh", [P, 4, 64], F32); r_mh = R()
        m1 = sb("m1", [P, 64], F32); m2 = sb("m2", [P, 64], F32); m3 = sb("m3", [P, 64], F32); r_m = R()
        rst = sb("rst", [P, 512], F32); r_rst = R()
        wbc_post = sb("wbc_post", [P, D], F32); r_wbc_post = R()
        wbc_fpost = sb("wbc_fpost", [P, D], F32); r_wbc_fpost = R()
        wbc_pre = sb("wbc_pre", [P, D], F32); r_wbc_pre = R()
        wbc_fpre = sb("wbc_fpre", [P, D], F32); r_wbc_fpre = R()
        rows = sb("rows", [32, P], F32); r_rows = R()
        pv = sb("pv", [P, 32], F32); r_pv = R()
        lbt = sb("lbt", [P, 8], F32); r_lb = R()
        pproj = sb("pproj", [P, 2, D], BF16); r_pproj = R()

        MEMSET("pool", ident[:], 0.0, [r_ident])
        pg.op("pool", lambda e: e.affine_select(out=ident[:], in_=ident[:], compare_op=ALU.not_equal, fill=1.0, base=0,
                                                pattern=[[-1, P]], channel_multiplier=1), reads=[r_ident], writes=[r_ident])
        MEMSET("pool", ones_bf[:], 1.0, [r_ones])
        MEMSET("pool", zeros_bf[:], 0.0, [r_zeros])
        MEMSET("pool", tmpf[:], 1.0, [r_tmpf])
        pg.op("pool", lambda e: e.affine_select(out=tmpf[:], in_=tmpf[:], compare_op=ALU.is_ge, fill=0.0, base=0,
                                                pattern=[[-1, P]], channel_multiplier=1), reads=[r_tmpf], writes=[r_tmpf])
        CP("pool", tri_bf[:], tmpf[:], [r_tmpf], [r_tri])
        MEMSET("pool", tmpf[:], 1.0, [r_tmpf], reads=[r_tmpf])
        pg.op("pool", lambda e: e.affine_select(out=tmpf[:], in_=tmpf[:], compare_op=ALU.is_gt, fill=0.0, base=0,
                                                pattern=[[1, P]], channel_multiplier=-1), reads=[r_tmpf], writes=[r_tmpf])
        CP("pool", su_bf[:], tmpf[:], [r_tmpf], [r_su])
        CP("pool", cmask[:], tmpf[:], [r_tmpf], [r_cmask])
        MEMSET("pool", m1[:], 1.0, [r_m])
        MEMSET("pool", m2[:], 1.0, [r_m])
        MEMSET("pool", m3[:], 1.0, [r_m])
        pg.op("pool", lambda e: e.affine_select(out=m1[:], in_=m1[:], compare_op=ALU.is_ge, fill=0.0, base=0,
                                                pattern=[[1, 64]], channel_multiplier=-1), reads=[r_m], writes=[r_m])
        pg.op("pool", lambda e: e.affine_select(out=m2[:], in_=m2[:], compare_op=ALU.is_ge, fill=0.0, base=64,
                                                pattern=[[1, 64]], channel_multiplier=-1), reads=[r_m], writes=[r_m])
        pg.op("pool", lambda e: e.affine_select(out=m3[:], in_=m3[:], compare_op=ALU.is_ge, fill=0.0, base=-64,
                                                pattern=[[0, 64]], channel_multiplier=1), reads=[r_m], writes=[r_m])
        TT_("pool", m2[:], m2[:], m3[:], ALU.mult, [r_m], [r_m])
        TT_("pool", m1[:], m1[:], m2[:], ALU.add, [r_m], [r_m])
        for m in range(4):
            CP("pool", mh[:, m, :], m1[:], [r_m], [r_mh])
        MEMSET("pool", rst[:], 1.0, [r_rst])
        MEMSET("pool", rst[:].rearrange("p (c t) -> p c t", t=64)[:, :, 0:1], 0.0, [r_rst], reads=[r_rst])

        pg.dma("sp", wbc_post[:], attn_post.partition_broadcast(P), writes=[r_wbc_post])
        pg.dma("sp", wbc_fpost[:], ffn_post.partition_broadcast(P), writes=[r_wbc_fpost])
        pg.dma("sp", wbc_pre[:], attn_pre.partition_broadcast(P), writes=[r_wbc_pre])
        pg.dma("sp", wbc_fpre[:], ffn_pre.partition_broadcast(P), writes=[r_wbc_fpre])
        pg.dma("sp", rows[0:8, :], hg_gamma.rearrange("r (c p) -> (r c) p", p=P), writes=[r_rows])
        r_rows2 = R(); r_rows3 = R(); r_rows4 = R(); r_rows5 = R()
        pg.dma("sp", rows[8:12, :], hg_norm.rearrange("(c p) -> c p", p=P), writes=[r_rows2])
        pg.dma("sp", rows[12:16, :], sb_norm.rearrange("(c p) -> c p", p=P), writes=[r_rows3])
        pg.dma("sp", rows[16:24, :], attn_pre.rearrange("(c p) -> c p", p=P), writes=[r_rows4])
        pg.dma("sp", rows[24:32, :], ffn_pre.rearrange("(c p) -> c p", p=P), writes=[r_rows5])
        identf32 = sb("identf32", [32, 32], F32); r_identf32 = R()
        MEMSET("pool", identf32[:], 0.0, [r_identf32])
        pg.op("pool", lambda e: e.affine_select(out=identf32[:], in_=identf32[:], compare_op=ALU.not_equal, fill=1.0, base=0,
                                                pattern=[[-1, 32]], channel_multiplier=1), reads=[r_identf32], writes=[r_identf32])
        MM(bk[1][:, 0:32], rows[:, :], identf32[:, :], True, True,
           [r_rows, r_rows2, r_rows3, r_rows4, r_rows5, r_identf32], [rb[1]])
        CP("dve", pv[:], bk[1][:, 0:32], [rb[1]], [r_pv])
        TT_("dve", lbt[:, 0:4], pv[:, 0:4], pv[:, 4:8], ALU.subtract, [r_pv], [r_lb])
        ACT(AF.Sigmoid, lbt[:, 0:4], lbt[:, 0:4], [r_lb], [r_lb])
        TS("dve", lbt[:, 4:8], lbt[:, 0:4], -1.0, 1.0, ALU.mult, ALU.add, [r_lb], [r_lb])

        cast_res = {}

        def cast(tag, src, dst, r0, r1, c0, c1):
            rr = R("wb_" + tag)
            w = c1 - c0
            bw = 256 if w % 256 == 0 else 128
            pg.dma("pool", dst[r0:r1, c0:c1].rearrange("r (a b) -> r a b", b=bw), src[r0:r1, c0:c1].rearrange("r (a b) -> r a b", b=bw),
                   writes=[rr])
            cast_res[tag] = [rr]

        IN_ORDER = [("hi", 1024), ("hg", 1536), ("sq", 2048), ("sk", 2560), ("sv", 3072), ("hq", 0), ("hf", 512)]
        GROUPS = [(0, 4), (4, 4), (8, 4), (12, 4), (16, 4), (20, 2)]
        DGRP = [(0, 8), (8, 8), (16, 6)]
        for nm, c0 in IN_ORDER:
            cast("in_" + nm, w_in, w_in_b, 0, D, c0, c0 + 512)
        cast("pproj", ple_proj, ple_proj_b, 0, PLE, 0, D)
        for half in range(2):
            cast("out%d" % half, w_out, w_out_b, 0, D, half * 512, (half + 1) * 512)
        for g, (f0, nf) in enumerate(GROUPS):
            cast("g%d" % g, w_gu, w_gu_b, 0, D, f0 * P, (f0 + nf) * P)
            cast("u%d" % g, w_gu, w_gu_b, 0, D, DFF + f0 * P, DFF + (f0 + nf) * P)
        for half in range(2):
            for fg, (f0, nf) in enumerate(DGRP):
                cast("d%d_%d" % (half, fg), w_down, w_down_b, f0 * P, (f0 + nf) * P, half * 512, (half + 1) * 512)
        for half in range(2):
            cast("pg%d" % half, ple_gate, ple_gate_b, 0, D, half * 512, (half + 1) * 512)

        pg.dma("sp", pproj[:], ple_proj_b.rearrange("(kc p) n -> p kc n", p=P), reads=cast_res["pproj"], writes=[r_pproj])

        NB = T // P
        skT = sb("skT", [P, 4, T], BF16); r_skT = [R() for _ in range(NT)]
        svb = sb("svb", [P, NB, SBW], BF16); r_sv = [R() for _ in range(NT)]
        xt = sb("xt", [P, 4, D], F32); r_xt = [R() for _ in range(4)]
        xn = [sb("xn%d" % i, [P, D], BF16) for i in range(2)]; r_xn = [R(), R()]
        junk = xn[1]; r_junk = r_xn[1]
        ssq = sb("ssq", [P, 8], F32); r_ssq = R()
        rstd = sb("rstd", [P, 8], F32); r_rstd = R()
        uT = sb("uT", [P, 8, TT], BF16); r_uT = [R() for _ in range(8)]
        catH = sb("catH", [P, 4, TT], BF16); r_catH = [R() for _ in range(4)]
        NSL = 3
        hsl = [sb("hsl%d" % i, [P, 8, 512], BF16) for i in range(NSL)]; r_h = [R() for _ in range(NSL)]
        Vt = sb("Vt", [P, 4, HGW], BF16); r_V = R()
        hgT = sb("hgT", [P, 4, TT], BF16); r_hgT = R()
        sqT = sb("sqT", [P, 4, TT], BF16); r_sqT = R()
        carry = sb("carry", [P, 4, P], F32); r_carry = R()
        sqs = sb("sqs", [P, TT], BF16); r_sqs = R()
        sqh = sqs; r_sqh = r_sqs
        catS = sb("catS", [P, 4, TT], BF16); r_catS = [R() for _ in range(4)]
        lnr = sb("lnr", [P, TT], F32); r_lnr = R()
        pt = sb("pt", [P, PLE], F32); r_pt = R()
        pn = sb("pn", [P, PLE], BF16); r_pn = R()
        pT = sb("pT", [P, 2, TT], BF16); r_pT = R()
        small = sb("small", [P, 32], F32); r_small = R()
        small_b = sb("small_b", [P, 24], F32); r_small_b = R()
        smalls = [(small, r_small), (small_b, r_small_b)]
        attn_scr = sb("attn_scr", [P, 6144], BF16)
        e_all = attn_scr[:, 0:2048].rearrange("p (a c t) -> p a c t", a=2, c=2)
        L_all = attn_scr[:, 2048:4096].rearrange("p (a c t) -> p a c t", a=2, c=2)
        X_one = attn_scr[:, 4096:5120].rearrange("p (c t) -> p c t", c=2)
        A_one = attn_scr[:, 5120:6144].rearrange("p (c t) -> p c t", c=2)
        e3 = sb("e3", [P, 2, 512], BF16)
        e_bufs = [e_all[:, 0], e_all[:, 1], e3[:]]
        r_e = [R(), R(), R()]; r_L = [R(), R()]; r_X = R(); r_A = R()
        ybuf = attn_scr[:, 0:4096].bitcast(F32).rearrange("p (j t) -> p j t", j=4); r_ybuf = R()
        tmp = attn_scr[:, 4096:5120].bitcast(F32); r_tmp = R()
        sg = attn_scr[:, 5120:6144].bitcast(F32); r_sg = R()
        actT = sb("actT", [P, 22, TT], BF16); r_actT = R()
        scr_f = actT[:].rearrange("p a b -> p (a b)").bitcast(F32)
        scr_b = actT[:].rearrange("p a b -> p (a b)")
        t1 = scr_f[:, 0:512]; t2 = scr_f[:, 512:1024]; t3 = scr_f[:, 1024:1536]
        qf = scr_f[:, 1536:2048]; ff = scr_f[:, 2048:2560]
        bo = 2560 * 2
        Qt_bf = scr_b[:, bo:bo + 512]; Qp_bf = scr_b[:, bo + 512:bo + 1024]
        Kt_bf = scr_b[:, bo + 1024:bo + 1536]; Kh_bf = scr_b[:, bo + 1536:bo + 2048]
        KhT = scr_b[:, bo + 2048:bo + 2560].rearrange("p (a b) -> p a b", b=P)
        scT = scr_b[:, bo + 2560:bo + 2816]
        Sbf = scr_b[:, bo + 2816:bo + 2816 + 1024].rearrange("p (a b) -> p a b", b=P)
        Sall = scr_f[:, 4480:5632].rearrange("p (a b) -> p a b", b=P); r_Sall = R()
        r_t1 = R(); r_t2 = R(); r_t3 = R(); r_qf = R(); r_ff = R()
        r_Qt = R(); r_Qp = R(); r_Kt = R(); r_Kh = R(); r_KhT = R(); r_scT = R(); r_Sbf = R()
        hg_scr = [r_t1, r_t2, r_t3, r_qf, r_ff, r_Qt, r_Qp, r_Kt, r_Kh, r_KhT, r_scT, r_Sbf, r_Sall]
        fence_t = sb("fence_t", [P, 2], F32)
        fence_set = hg_scr + [r_actT, r_ybuf, r_tmp, r_sg, r_X, r_A] + r_e + r_L

        def fence():
            MEMSET("pool", fence_t[:, 0:1], 0.0, fence_set)

        MEMSET("pool", carry[:], 0.0, [r_carry])

        wv_in = w_in_b.rearrange("(kc p) n -> p kc n", p=P)
        wv_out = w_out_b.rearrange("(kc p) n -> p kc n", p=P)
        wv_gu = w_gu_b.rearrange("(kc p) n -> p kc n", p=P)
        wv_dn = w_down_b.rearrange("(kc p) n -> p kc n", p=P)
        wv_pg = ple_gate_b.rearrange("(kc p) n -> p kc n", p=P)
        def tile_blocks():
            bl = []
            for nm, c0 in IN_ORDER:
                bl.append(("in_" + nm, wv_in[:, :, c0:c0 + 512], 8, 512, cast_res["in_" + nm]))
            for half in range(2):
                bl.append(("out%d" % half, wv_out[:, :, half * 512:(half + 1) * 512], 8, 512, cast_res["out%d" % half]))
            for g, (f0, nf) in enumerate(GROUPS):
                bl.append(("g%d" % g, wv_gu[:, :, f0 * P:(f0 + nf) * P], 8, nf * P, cast_res["g%d" % g]))
                bl.append(("u%d" % g, wv_gu[:, :, DFF + f0 * P:DFF + (f0 + nf) * P], 8, nf * P, cast_res["u%d" % g]))
            for half in range(2):
                for fg, (f0, nf) in enumerate(DGRP):
                    bl.append(("d%d_%d" % (half, fg), wv_dn[:, f0:f0 + nf, half * 512:(half + 1) * 512], nf, 512, cast_res["d%d_%d" % (half, fg)]))
            for half in range(2):
                bl.append(("pg%d" % half, wv_pg[:, :, half * 512:(half + 1) * 512], 8, 512, cast_res["pg%d" % half]))
            return bl

        blocks = []
        for ti in range(NT):
            blocks.extend(tile_blocks())
        wstate = {"issued": 0, "next": 0, "released": 0}

        def _issue_upto(n):
            while wstate["issued"] < min(len(blocks), n):
                j = wstate["issued"]
                assert j - NSL < wstate["released"], "slot still live"
                _, view, nk, ncol, rsrc = blocks[j]
                pg.dma("sp", hsl[j % NSL][:, 0:nk, 0:ncol], view, reads=rsrc, writes=[r_h[j % NSL]])
                wstate["issued"] += 1

        def wnext(tag):
            i = wstate["next"]
            wstate["next"] += 1
            assert blocks[i][0] == tag, (blocks[i][0], tag)
            _issue_upto(i + 1)
            return hsl[i % NSL], r_h[i % NSL]

        def wrel(k=1):
            wstate["released"] += k
            assert wstate["released"] <= wstate["next"]
            _issue_upto(wstate["released"] + NSL)

        bank_rot = {"i": 0}

        def nextbank(choices):
            b = choices[bank_rot["i"] % len(choices)]
            bank_rot["i"] += 1
            return b

        def fm_mm(b, W, rW, col0, ncols=P):
            for k in range(8):
                MM(bk[b][0:ncols, :], W[:, k, col0:col0 + ncols], uT[:, k, :], k == 0, k == 7, [rW, r_uT[k]], [rb[b]])

        def tm_mm(b, j, W, rW, col0, ncols=512):
            for k in range(8):
                MM(bk[b][:, 0:ncols], uT[:, k, j * P:(j + 1) * P], W[:, k, col0:col0 + ncols], k == 0, k == 7,
                   [rW, r_uT[k]], [rb[b]])

        bkT2 = psall[:, 7 * 512:8 * 512].bitcast(BF16)
        tr_banks = [(bkT, rb[0]), (bkT2, rb[7])]

        def norm_transpose(wbc, r_wbc, src_is_norm=True):
            if src_is_norm:
                for j in range(4):
                    ACT(AF.Square, junk[:], xt[:, j, :], [r_xt[j]], [r_junk, r_ssq], accum_out=ssq[:, j:j + 1])
                    ACT(AF.Ln, rstd[:, j:j + 1], ssq[:, j:j + 1], [r_ssq], [r_rstd], scale=1.0 / D, bias=EPS)
                    ACT(AF.Exp, rstd[:, j:j + 1], rstd[:, j:j + 1], [r_rstd], [r_rstd], scale=-0.5)
            for j in range(4):
                xb = j % 2
                tb, rtb = tr_banks[j % 2]
                if src_is_norm:
                    STT("dve", xn[xb][:], xt[:, j, :], rstd[:, j:j + 1], wbc[:], ALU.mult, ALU.mult, [r_xt[j], r_rstd, r_wbc], [r_xn[xb]])
                else:
                    CP("act" if j % 2 == 0 else "dve", xn[xb][:], xt[:, j, :], [r_xt[j]], [r_xn[xb]])
                for k in range(8):
                    pg.op("pe", lambda e, k=k, xb=xb, tb=tb: e.transpose(out=tb[:, k * P:(k + 1) * P], in_=xn[xb][:, k * P:(k + 1) * P],
                                                                          identity=ident[:]), reads=[r_xn[xb], r_ident], writes=[rtb])
                CP("act" if j % 2 == 0 else "dve", uT[:, :, j * P:(j + 1) * P], tb.rearrange("p (k t) -> p k t", k=8), [rtb], r_uT)

        acc_hg = xn[0][:].bitcast(F32)
        acc_sb = xn[1][:].bitcast(F32)

        def bc_rstd(acc, r_acc, width, buf, rbuf):
            ACT(AF.Ln, lnr[:], acc, [r_acc], [r_lnr], scale=1.0 / width, bias=EPS)
            ACT(AF.Exp, lnr[:], lnr[:], [r_lnr], [r_lnr], scale=-0.5)
            for c in range(4):
                TT_("dve", buf[:, c, :], buf[:, c, :], lnr[:], ALU.mult, [rbuf[c], r_lnr], [rbuf[c]])

        for ti in range(NT):
            tok0 = ti * TT
            if ti == 0:
                for j in range(4):
                    pg.dma("sp", xt[:, j, :], x[tok0 + j * P:tok0 + (j + 1) * P, :], writes=[r_xt[j]])
            norm_transpose(wbc_pre, r_wbc_pre)
            TAP('uT1', uT[:], r_uT, ti)
            fence()
            W, rW = wnext("in_hi")
            for j in range(4):
                b = nextbank([1, 2])
                tm_mm(b, j, W, rW, 0)
                CP("act", Vt[:, j, :], bk[b][:], [rb[b]], [r_V])
            wrel()
            W, rW = wnext("in_hg")
            for c in range(4):
                b = nextbank([1, 2])
                fm_mm(b, W, rW, c * P)
                ACT(AF.Silu, hgT[:, c, :], bk[b][:], [rb[b]], [r_hgT])
            wrel()
            W, rW = wnext("in_sq")
            for c in range(4):
                b = nextbank([1, 2])
                fm_mm(b, W, rW, c * P)
                CP("dve", sqT[:, c, :], bk[b][:], [rb[b]], [r_sqT])
            wrel()
            W, rW = wnext("in_sk")
            for c in range(4):
                b = nextbank([1, 2])
                fm_mm(b, W, rW, c * P)
                CP("dve", skT[:, c, tok0:tok0 + TT], bk[b][:], [rb[b]], [r_skT[ti]])
            wrel()
            W, rW = wnext("in_sv")
            for j in range(4):
                b = nextbank([1, 2])
                tm_mm(b, j, W, rW, 0)
                CP("act", svb[:, 4 * ti + j, :], bk[b][:], [rb[b]], [r_sv[ti]])
            wrel()
            TAP('Vt', Vt[:], [r_V], ti); TAP('hgT', hgT[:], [r_hgT], ti); TAP('sqT', sqT[:], [r_sqT], ti)
            TAP('skT', skT[:, :, 0:512], [r_skT[0]], ti); TAP('sv', svb[:, 0:4, :], [r_sv[0]], ti)
            Wq, rWq = wnext("in_hq")
            Wf, rWf = wnext("in_hf")
            b3 = lambda ap: ap.rearrange("p (c t) -> p c t", t=64)
            HQ, HF, HX = 2, 3, 7
            bkX = psall[:, HX * 512:(HX + 1) * 512].bitcast(BF16)

            HINT = [2.5, 2.5, 3.0, 3.0, 1.5, 2.0, 2.0, 2.0, 2.0, 1.5, 1.0, 2.0, 2.0, 1.0, 1.5, 2.0, 1.5, 2.5, 1.5, 1.0, 1.0]

            def hg_gen():
                for h in range(4):
                    sm, r_sm = smalls[h % 2]
                    for k in range(8):
                        MM(bk[HQ][:, :], Wq[:, k, h * P:(h + 1) * P], uT[:, k, :], k == 0, k == 7, [rWq, r_uT[k]], [rb[HQ]])
                        if k == 3:
                            yield 1.3
                    yield HINT[0]
                    for k in range(8):
                        MM(bk[HF][:, :], Wf[:, k, h * P:(h + 1) * P], uT[:, k, :], k == 0, k == 7, [rWf, r_uT[k]], [rb[HF]])
                        if k == 3:
                            ACT(AF.Exp, t3, bk[HQ][:], [rb[HQ]], [r_t3], scale=-1.0)
                            yield 1.3
                    yield HINT[1]
                    ACT(AF.Exp, t1, bk[HF][:], [rb[HF]], [r_t1], scale=-1.0)
                    TS("dve", t3, t3, 1.0, None, ALU.add, None, [r_t3], [r_t3])
                    pg.op("dve", lambda e: e.reciprocal(out=t3, in_=t3), reads=[r_t3], writes=[r_t3])
                    TT_("dve", qf, bk[HQ][:], t3, ALU.mult, [rb[HQ], r_t3], [r_qf])
                    yield HINT[2]
                    TS("dve", t1, t1, 1.0, None, ALU.add, None, [r_t1], [r_t1])
                    pg.op("dve", lambda e: e.reciprocal(out=t1, in_=t1), reads=[r_t1], writes=[r_t1])
                    TS("dve", ff, t1, lbt[:, 4 + h:5 + h], lbt[:, h:h + 1], ALU.mult, ALU.add, [r_t1, r_lb], [r_ff])
                    yield HINT[3]
                    yield 3.0
                    ACT(AF.Ln, t1, ff, [r_ff], [r_t1])
                    TS("pool", ff, ff, -1.0, 1.0, ALU.mult, ALU.add, [r_ff], [r_ff])
                    yield HINT[4]
                    pg.op("dve", lambda e: e.tensor_tensor_scan(out=t2, data0=rst[:], data1=t1, initial=0.0, op0=ALU.mult, op1=ALU.add),
                          reads=[r_rst, r_t1], writes=[r_t2])
                    TT_("dve", b3(t1), b3(t2), b3(t2)[:, :, 31:32].broadcast_to([P, 8, 64]), ALU.subtract, [r_t2], [r_t1])
                    yield HINT[5]
                    yield 2.5
                    ACT(AF.Exp, sm[:, 0:8], b3(t2)[:, :, 31], [r_t2], [r_sm])
                    ACT(AF.Exp, t3, t1, [r_t1], [r_t3])
                    yield HINT[6]
                    ACT(AF.Exp, t1, t1, [r_t1], [r_t1], scale=-1.0)
                    CP("pool", sm[:, 8:16], b3(t3)[:, :, 63], [r_t3], [r_sm])
                    TT_("pool", sm[:, 16:24], sm[:, 8:16], sm[:, 0:8], ALU.mult, [r_sm], [r_sm])
                    TT_("dve", qf, qf, t3, ALU.mult, [r_qf, r_t3], [r_qf])
                    yield HINT[7]
                    TT_("pool", ff, ff, t1, ALU.mult, [r_ff, r_t1], [r_ff])
                    CP("pool", Qt_bf, qf, [r_qf], [r_Qt])
                    TT_("dve", b3(Qp_bf), b3(qf), sm[:, 0:8].unsqueeze(2).broadcast_to([P, 8, 64]), ALU.mult, [r_qf, r_sm], [r_Qp])
                    yield HINT[8]
                    CP("pool", Kt_bf, ff, [r_ff], [r_Kt])
                    TT_("dve", b3(Kh_bf), b3(ff), sm[:, 8:16].unsqueeze(2).broadcast_to([P, 8, 64]), ALU.mult, [r_ff, r_sm], [r_Kh])
                    yield HINT[9]
                    yield HINT[10]
                    for blk in range(4):
                        pg.op("pe", lambda e, blk=blk: e.transpose(out=bkX[:, blk * P:(blk + 1) * P], in_=Kh_bf[:, blk * P:(blk + 1) * P],
                                                                    identity=ident[:]), reads=[r_Kh, r_ident], writes=[rb[HX]])
                    yield 1.0
                    for c in range(8):
                        m, pr = c // 2, 64 * (c % 2)
                        MM(bk[HQ][pr:pr + 64, m * 64:(m + 1) * 64], Kt_bf[:, c * 64:(c + 1) * 64], Qt_bf[:, c * 64:(c + 1) * 64],
                           True, True, [r_Kt, r_Qt], [rb[HQ]])
                    yield HINT[11]
                    CP("dve", KhT, bkX[:, 0:512].rearrange("p (a b) -> p a b", b=P), [rb[HX]], [r_KhT])
                    TT_("dve", scT, bk[HQ][:, 0:256], mh[:].rearrange("p a b -> p (a b)"), ALU.mult, [rb[HQ], r_mh], [r_scT])
                    CP("pool", Sall[:, 0, :], carry[:, h, :], [r_carry], [r_Sall])
                    yield HINT[12]
                    yield HINT[13]
                    for c in range(8):
                        m, pr = c // 2, 64 * (c % 2)
                        ub = HF if c % 2 == 0 else HX
                        MM(bk[ub][:, m * P:(m + 1) * P], KhT[pr:pr + 64, m, :], Vt[pr:pr + 64, m, h * P:(h + 1) * P],
                           True, True, [r_KhT, r_V], [rb[ub]])
                    yield HINT[14]
                    for c in range(8):
                        m = c // 2
                        ub = HF if c % 2 == 0 else HX
                        STT("dve", Sall[:, c + 1, :], Sall[:, c, :], sm[:, 16 + c:17 + c], bk[ub][:, m * P:(m + 1) * P],
                            ALU.mult, ALU.add, [r_Sall, r_sm, rb[ub]], [r_Sall])
                        if c % 2 == 1:
                            yield 2.5
                    CP("pool", carry[:, h, :], Sall[:, 8, :], [r_Sall], [r_carry])
                    CP("pool", Sbf, Sall[:, 0:8, :], [r_Sall], [r_Sbf])
                    yield HINT[15]
                    yield HINT[16]
                    for c in range(8):
                        m, pr = c // 2, 64 * (c % 2)
                        MM(bk[HQ][:, c * 64:(c + 1) * 64], Sbf[:, c, :], Qp_bf[:, c * 64:(c + 1) * 64], True, False,
                           [r_Sbf, r_Qp], [rb[HQ]])
                        MM(bk[HQ][:, c * 64:(c + 1) * 64], Vt[pr:pr + 64, m, h * P:(h + 1) * P], scT[pr:pr + 64, m * 64:(m + 1) * 64],
                           False, True, [r_V, r_scT], [rb[HQ]])
                        if c == 3:
                            yield 1.0
                    yield HINT[17]
                    STT("dve", catH[:, h, :], bk[HQ][:], pv[:, 8 + h:9 + h], hgT[:, h, :], ALU.mult, ALU.mult,
                        [rb[HQ], r_pv, r_hgT], [r_catH[h]])
                    ACT(AF.Square, sqh[:], bk[HQ][:], [rb[HQ]], [r_sqh])
                    yield HINT[18]
                    MM(bk[HF][:], ones_bf[:], sqh[:], True, True, [r_ones, r_sqh], [rb[HF]])
                    yield HINT[19]
                    if h == 0:
                        CP("dve", acc_hg, bk[HF][:], [rb[HF]], [r_xn[0]])
                    else:
                        TT_("dve", acc_hg, acc_hg, bk[HF][:], ALU.add, [rb[HF], r_xn[0]], [r_xn[0]])
                    yield HINT[20]

            hgen = hg_gen()
            hg_state = {"done": False, "budget": 0.0, "need": 0.0}

            def hg_advance(dt):
                hg_state["budget"] += dt
                while not hg_state["done"] and hg_state["budget"] >= hg_state["need"]:
                    hg_state["budget"] -= hg_state["need"]
                    try:
                        hg_state["need"] = next(hgen)
                    except StopIteration:
                        hg_state["done"] = True

            kbs = list(range(4 * ti + 3, -1, -1))
            nst = len(kbs)
            SLOT_US = 3.3
            HG_TOTAL = 4 * (sum(HINT) + 4 * 2.5 + 1.3 * 2 + 2.0 + 5.5)
            per_slot = SLOT_US * max(1.0, HG_TOTAL / (4 * nst * SLOT_US))
            psC2 = psall[:, 4 * 512:6 * 512].rearrange("p (c t) -> p c t", c=2)
            psZ2 = psall[:, 0:1024].rearrange("p (c t) -> p c t", c=2)

            def t0_of(k):
                return max(0, kbs[k] - 4 * ti) * P

            NG = 4 * nst

            def Zs(g):
                pp, k = divmod(g, nst)
                kb, t0 = kbs[k], t0_of(k)
                for hh in range(2):
                    pb = 64 * hh
                    MM(bk[hh][:, t0:512], skT[pb:pb + 64, pp, kb * P:(kb + 1) * P], sqT[pb:pb + 64, pp, t0:512], True, True,
                       [r_skT[kb // 4], r_sqT], [rb[hh]])

            def Es(g):
                pp, k = divmod(g, nst)
                kb, pe3, t0 = kbs[k], g % 3, t0_of(k)
                eb = e_bufs[pe3]
                ACT(AF.Exp, eb[:, :, t0:512], psZ2[:, :, t0:512], [rb[0], rb[1]], [r_e[pe3]], scale=0.125)
                if kb >= 4 * ti:
                    TT_("pool", eb[:, :, t0:t0 + P], eb[:, :, t0:t0 + P],
                        cmask[:].unsqueeze(1).broadcast_to([P, 2, P]), ALU.mult, [r_e[pe3], r_cmask], [r_e[pe3]])

            def Ls(g):
                k = g % nst
                par, t0 = g % 2, t0_of(k)
                ACT(AF.Ln, L_all[:, par, :, t0:512], e_bufs[g % 3][:, :, t0:512], [r_e[g % 3]], [r_L[par]], bias=1.0)

            def Ts(g):
                k = g % nst
                par, t0 = g % 2, t0_of(k)
                if k == 0:
                    for _z in range(4):
                        MM(bk[4][:, _z * P:(_z + 1) * P], zeros_bf[:], zeros_bf[:], _z == 0, False, [r_zeros], [rb[4]], skip=True)
                    for _z in range(4):
                        MM(bk[5][:, _z * P:(_z + 1) * P], zeros_bf[:], zeros_bf[:], _z == 0, False, [r_zeros], [rb[5]], skip=True)
                for hh in range(2):
                    MM(bk[4 + hh][:, t0:512], tri_bf[:], L_all[:, par, hh, t0:512], False, False, [r_tri, r_L[par]], [rb[4 + hh]], skip=True)

            def Xs(g):
                k = g % nst
                t0 = t0_of(k)
                ACT(AF.Exp, X_one[:, :, t0:512], psC2[:, :, t0:512], [rb[4], rb[5]], [r_X], scale=-1.0)

            def Ss(g):
                k = g % nst
                par, t0 = g % 2, t0_of(k)
                if k == nst - 1:
                    return
                for hh in range(2):
                    MM(bk[4 + hh][:, t0:512], su_bf[:], L_all[:, par, hh, t0:512], False, False, [r_su, r_L[par]], [rb[4 + hh]], skip=True)

            def As(g):
                k = g % nst
                par, t0 = g % 2, t0_of(k)
                TT_("dve", A_one[:, :, t0:512], e_bufs[g % 3][:, :, t0:512], X_one[:, :, t0:512], ALU.mult,
                    [r_e[g % 3], r_X], [r_A])

            def Vs(g):
                pp, k = divmod(g, nst)
                kb, t0 = kbs[k], t0_of(k)
                if k == 0:
                    for _z in range(4):
                        MM(bk[6][:, _z * P:(_z + 1) * P], zeros_bf[:], zeros_bf[:], _z == 0, False, [r_zeros], [rb[6]], skip=True)
                for hh in range(2):
                    hd = 2 * pp + hh
                    pb = 64 * hh
                    MM(bk[6][pb:pb + 64, t0:512], svb[:, kb, hd * 64:(hd + 1) * 64], A_one[:, hh, t0:512], False, False,
                       [r_sv[kb // 4], r_A], [rb[6]], skip=True)
                if k == nst - 1:
                    TS("dve", catS[:, pp, :], bk[6][:], pv[:, 12 + pp:13 + pp], None, ALU.mult, None, [rb[6], r_pv], [r_catS[pp]])
                    ACT(AF.Square, sqs[:], bk[6][:], [rb[6]], [r_sqs])
                    MM(bk[6][:], ones_bf[:], sqs[:], True, True, [r_ones, r_sqs], [rb[6]])
                    if pp == 0:
                        CP("dve", acc_sb, bk[6][:], [rb[6]], [r_xn[1]])
                    else:
                        TT_("dve", acc_sb, acc_sb, bk[6][:], ALU.add, [rb[6], r_xn[1]], [r_xn[1]])

            Zs(0)
            Es(0)
            Ls(0)
            if NG > 1:
                Zs(1)
            for g in range(NG):
                Ts(g)
                if g >= 1:
                    Vs(g - 1)
                if g + 1 < NG:
                    Es(g + 1)
                if g + 2 < NG:
                    Zs(g + 2)
                Xs(g)
                As(g)
                hg_advance(per_slot)
                if g + 1 < NG:
                    Ls(g + 1)
                Ss(g)
            Vs(NG - 1)
            hg_advance(10 ** 9)
            assert hg_state["done"]
            wrel(2)
            TAP('cat_hg_raw', catH[:], r_catH, ti)
            bc_rstd(acc_hg, r_xn[0], HGW, catH, r_catH)
            TAP('cat_hg', catH[:], r_catH, ti)
            TAP('cat_sb_raw', catS[:], r_catS, ti)
            bc_rstd(acc_sb, r_xn[1], SBW, catS, r_catS)
            TAP('cat_sb', catS[:], r_catS, ti)
            fence()
            W0, rW0 = wnext("out0")
            W1, rW1 = wnext("out1")
            for j in range(4):
                b0, b1 = (1, 2) if j % 2 == 0 else (3, 4)
                for half, (bb, W, rW) in enumerate(((b0, W0, rW0), (b1, W1, rW1))):
                    for k in range(8):
                        src, rsrc = (catH[:, k, j * P:(j + 1) * P], r_catH[k]) if k < 4 else (catS[:, k - 4, j * P:(j + 1) * P], r_catS[k - 4])
                        MM(bk[bb][:], src, W[:, k, :], k == 0, k == 7, [rW, rsrc], [rb[bb]])
                    ACT(AF.Square, junk[:, 0:512], bk[bb][:], [rb[bb]], [r_junk, r_ssq], accum_out=ssq[:, 4 + half:5 + half])
                TT_("dve", ssq[:, 6:7], ssq[:, 4:5], ssq[:, 5:6], ALU.add, [r_ssq], [r_ssq])
                ACT(AF.Ln, rstd[:, 4:5], ssq[:, 6:7], [r_ssq], [r_rstd], scale=1.0 / D, bias=EPS)
                ACT(AF.Exp, rstd[:, 4:5], rstd[:, 4:5], [r_rstd], [r_rstd], scale=-0.5)
                for half, bb in enumerate((b0, b1)):
                    tb_, rtb_ = (tmp, r_tmp) if half == 0 else (lnr[:], r_lnr)
                    STT("dve", tb_, bk[bb][:], rstd[:, 4:5], wbc_post[:, half * 512:(half + 1) * 512], ALU.mult, ALU.mult,
                        [rb[bb], r_rstd, r_wbc_post], [rtb_])
                    TT_("pool", xt[:, j, half * 512:(half + 1) * 512], xt[:, j, half * 512:(half + 1) * 512], tb_, ALU.add,
                        [rtb_, r_xt[j]], [r_xt[j]])
            wrel(2)
            TAP('h1', xt[:], r_xt, ti)
            norm_transpose(wbc_fpre, r_wbc_fpre)
            fence()
            gi = 0
            for g, (f0, nf) in enumerate(GROUPS):
                Wg, rWg = wnext("g%d" % g)
                Wu, rWu = wnext("u%d" % g)
                for c in range(nf):
                    bg, bu = (1, 2) if gi % 2 == 0 else (3, 4)
                    gi += 1
                    fm_mm(bg, Wg, rWg, c * P)
                    fm_mm(bu, Wu, rWu, c * P)
                    ACT(AF.Silu, sg, bk[bg][:], [rb[bg]], [r_sg])
                    TT_("dve", actT[:, f0 + c, :], bk[bu][:], sg, ALU.mult, [rb[bu], r_sg], [r_actT])
                wrel(2)
            for fg, (f0, nf) in enumerate(DGRP):
                W, rW = wnext("d0_%d" % fg)
                for j in range(4):
                    for c in range(nf):
                        fc = f0 + c
                        MM(bk[1 + j][:], actT[:, fc, j * P:(j + 1) * P], W[:, c, :], fc == 0, fc == 21, [rW, r_actT], [rb[1 + j]])
                wrel()
            for j in range(4):
                ACT(AF.Square, junk[:, 0:512], bk[1 + j][:], [rb[1 + j]], [r_junk, r_small], accum_out=small[:, 24 + j:25 + j])
                CP("dve", ybuf[:, j, :], bk[1 + j][:], [rb[1 + j]], [r_ybuf])
            Wd = [wnext("d1_%d" % fg) for fg in range(3)]
            for j in range(4):
                for fg, (f0, nf) in enumerate(DGRP):
                    W, rW = Wd[fg]
                    for c in range(nf):
                        fc = f0 + c
                        MM(bk[1 + j][:], actT[:, fc, j * P:(j + 1) * P], W[:, c, :], fc == 0, fc == 21, [rW, r_actT], [rb[1 + j]])
                ACT(AF.Square, junk[:, 0:512], bk[1 + j][:], [rb[1 + j]], [r_junk, r_small], accum_out=small[:, 28 + j:29 + j])
                TT_("dve", ssq[:, j:j + 1], small[:, 24 + j:25 + j], small[:, 28 + j:29 + j], ALU.add, [r_small], [r_ssq])
                ACT(AF.Ln, rstd[:, j:j + 1], ssq[:, j:j + 1], [r_ssq], [r_rstd], scale=1.0 / D, bias=EPS)
                ACT(AF.Exp, rstd[:, j:j + 1], rstd[:, j:j + 1], [r_rstd], [r_rstd], scale=-0.5)
                STT("dve", tmp, ybuf[:, j, :], rstd[:, j:j + 1], wbc_fpost[:, 0:512], ALU.mult, ALU.mult,
                    [r_ybuf, r_rstd, r_wbc_fpost], [r_tmp])
                TT_("pool", xt[:, j, 0:512], xt[:, j, 0:512], tmp, ALU.add, [r_tmp, r_xt[j]], [r_xt[j]])
                STT("dve", lnr[:], bk[1 + j][:], rstd[:, j:j + 1], wbc_fpost[:, 512:1024], ALU.mult, ALU.mult,
                    [rb[1 + j], r_rstd, r_wbc_fpost], [r_lnr])
                TT_("pool", xt[:, j, 512:1024], xt[:, j, 512:1024], lnr[:], ALU.add, [r_lnr, r_xt[j]], [r_xt[j]])
            wrel(3)
            TAP('h2', xt[:], r_xt, ti)
            TAP('actT', actT[:], [r_actT], ti)
            norm_transpose(None, None, src_is_norm=False)
            for j in range(4):
                pg.dma("sp", pt[:], p_in[tok0 + j * P:tok0 + (j + 1) * P, :], writes=[r_pt])
                CP("dve", pn[:], pt[:], [r_pt], [r_pn])
                for k in range(2):
                    pg.op("pe", lambda e, k=k: e.transpose(out=bkT[:, k * P:(k + 1) * P], in_=pn[:, k * P:(k + 1) * P],
                                                            identity=ident[:]), reads=[r_pn, r_ident], writes=[r_bkT])
                CP("act", pT[:, :, j * P:(j + 1) * P], bkT[:, 0:256].rearrange("p (a b) -> p a b", b=P), [r_bkT], [r_pT])
            Wg0, rWg0 = wnext("pg0")
            Wg1, rWg1 = wnext("pg1")
            for j in range(4):
                for half, (W, rW) in enumerate(((Wg0, rWg0), (Wg1, rWg1))):
                    bg, bp = (1, 2) if (2 * j + half) % 2 == 0 else (3, 4)
                    tm_mm(bg, j, W, rW, 0)
                    for k in range(2):
                        MM(bk[bp][:], pT[:, k, j * P:(j + 1) * P], pproj[:, k, half * 512:(half + 1) * 512], k == 0, k == 1,
                           [r_pT, r_pproj], [rb[bp]])
                    ACT(AF.Sigmoid, sg, bk[bg][:], [rb[bg]], [r_sg])
                    tb_, rtb_ = (tmp, r_tmp) if half == 0 else (lnr[:], r_lnr)
                    TT_("dve", tb_, bk[bp][:], sg, ALU.mult, [rb[bp], r_sg], [rtb_])
                    TT_("pool", tb_, tb_, xt[:, j, half * 512:(half + 1) * 512], ALU.add, [rtb_, r_xt[j]], [rtb_])
                    pg.dma("sp", y[tok0 + j * P:tok0 + (j + 1) * P, half * 512:(half + 1) * 512], tb_, reads=[rtb_])
                if ti + 1 < NT:
                    pg.dma("sp", xt[:, j, :], x[tok0 + TT + j * P:tok0 + TT + (j + 1) * P, :], writes=[r_xt[j]])
            wrel(2)
        pg.finish("sp", r_xt + [r_tmp, r_lnr] + tap_outs)
        if _os.environ.get('SBUF_REPORT'):
            print('sbuf remaining', nc.sbuf_bytes_remaining, {e: len(v) for e, v in pg.ops.items()})
        pg.emit()
    return nc


_NC_CACHE = {}


def kernel(**inputs):
    x = np.asarray(inputs["x"], dtype=np.float32)
    B, Tn, _ = x.shape
    if Tn not in _NC_CACHE:
        _NC_CACHE[Tn] = build(Tn)
    nc = _NC_CACHE[Tn]
    shared = {}
    for name in ("attn_pre_norm", "w_in", "hg_out_norm", "sb_out_norm", "w_out", "attn_post_norm", "ffn_pre_norm",
                 "w_gate_up", "w_down", "ffn_post_norm", "ple_proj", "ple_gate"):
        a = np.asarray(inputs[name], dtype=np.float32)
        shared[name] = np.ascontiguousarray(a[0])
    shared["hg_lower_gamma"] = np.ascontiguousarray(np.asarray(inputs["hg_lower_gamma"], dtype=np.float32))
    p = np.asarray(inputs["p"], dtype=np.float32)[0]
    in_maps = []
    for b in range(B):
        m = dict(shared)
        m["x"] = np.ascontiguousarray(x[b])
        m["p"] = np.ascontiguousarray(p[b])
        in_maps.append(m)
    res = run_bass_kernel_spmd(nc, in_maps, core_ids=list(range(B)))
    return np.stack([np.asarray(r["y"]) for r in res.results], axis=0).astype(np.float32)
```

```python
import contextlib
import os as _os
_DBGH = int(_os.environ.get('DBGH', '0'))
import numpy as np
import concourse.bass as bass
import concourse.mybir as mybir
from concourse.bass_utils import run_bass_kernel_spmd

F32 = mybir.dt.float32
BF16 = mybir.dt.bfloat16
AF = mybir.ActivationFunctionType
ALU = mybir.AluOpType
AX = mybir.AxisListType

D = 1024
T = 4096
PLE = 256
HGW = 512
SBW = 512
DFF = 2816
INC = 3584
EPS = 1e-6
TT = 512
NTT = T // TT
P = 128
NCORES = 8

ENGS = ("pe", "act", "dve", "pool", "sp")


class Res:
    __slots__ = ("name", "w", "r", "alias", "excl")

    def __init__(self, name, excl=False):
        self.name = name
        self.w = None
        self.r = {}
        self.alias = []
        self.excl = excl


class Ev:
    __slots__ = ("kind", "eng", "idx", "vc", "op")

    def __init__(self, kind, eng, idx, vc, op=None):
        self.kind = kind
        self.eng = eng
        self.idx = idx
        self.vc = vc
        self.op = op


class Op:
    __slots__ = ("eng", "idx", "fn", "waits", "signal", "dma", "name")

    def __init__(self, eng, idx, fn, name=""):
        self.eng = eng
        self.idx = idx
        self.fn = fn
        self.waits = []
        self.signal = False
        self.dma = None
        self.name = name


class Prog:
    def __init__(self, nc, stack, n_dma_sems=60, n_pool_sems=12):
        self.n_pool_sems = n_pool_sems
        self.next_dsem_q = {}
        self.nc = nc
        self.stack = stack
        self.ops = {e: [] for e in ENGS}
        self.vc = {e: {f: 0 for f in ENGS} for e in ENGS}
        self.seen_d = {e: {} for e in ENGS}
        self.sem = {e: stack.enter_context(nc.semaphore("s_" + e)) for e in ENGS}
        self.dsem = [stack.enter_context(nc.semaphore("d%d" % i)) for i in range(n_dma_sems)]
        self.dcnt = [0] * n_dma_sems
        self.res_dsem = {}
        self.next_dsem = 0
        self.nres = 0

    def res(self, name=None, excl=False):
        self.nres += 1
        return Res(name or ("r%d" % self.nres), excl)

    def alias(self, a, b):
        a.alias.append(b)
        b.alias.append(a)

    def _need(self, op, ev):
        e = op.eng
        if ev is None:
            return
        if ev.kind == "c":
            if ev.eng == e and e in ("pe", "sp"):
                return
            if self.vc[e][ev.eng] >= ev.idx + 1:
                return
            op.waits.append(ev)
            ev.op.signal = True
            self.vc[e][ev.eng] = ev.idx + 1
            for f, v in ev.vc.items():
                if self.vc[e][f] < v:
                    self.vc[e][f] = v
        else:
            slot, val = ev.eng, ev.idx
            if self.seen_d[e].get(slot, 0) >= val:
                return
            op.waits.append(ev)
            self.seen_d[e][slot] = val
            for f, v in ev.vc.items():
                if self.vc[e][f] < v:
                    self.vc[e][f] = v

    def _expand(self, lst):
        out = []
        for r in lst:
            out.append(r)
            out.extend(r.alias)
        return out

    def _deps(self, op, reads, writes):
        for r in reads:
            self._need(op, r.w)
        for r in writes:
            self._need(op, r.w)
            for ev in r.r.values():
                self._need(op, ev)

    def op(self, eng, fn, reads=(), writes=(), name=""):
        reads = self._expand(reads)
        writes = self._expand(writes)
        ex = [r for r in reads if r.excl]
        if ex:
            writes = writes + [r for r in ex if r not in writes]
            reads = [r for r in reads if not r.excl]
        lst = self.ops[eng]
        o = Op(eng, len(lst), fn, name)
        lst.append(o)
        self._deps(o, reads, writes)
        ev = Ev("c", eng, o.idx, dict(self.vc[eng]), o)
        for r in writes:
            r.w = ev
            r.r = {}
        for r in reads:
            r.r[eng] = ev
        return o

    def dma(self, q, out, in_, reads=(), writes=(), name="", **kw):
        reads = self._expand(reads)
        writes = self._expand(writes)
        lst = self.ops[q]
        o = Op(q, len(lst), None, name)
        lst.append(o)
        self._deps(o, reads, writes)
        key = (q, writes[0] if writes else reads[0])
        if key not in self.res_dsem:
            lo, n = (0, self.n_pool_sems) if q == "pool" else (self.n_pool_sems, len(self.dsem) - self.n_pool_sems)
            self.res_dsem[key] = lo + self.next_dsem_q.get(q != "pool", 0) % n
            self.next_dsem_q[q != "pool"] = self.next_dsem_q.get(q != "pool", 0) + 1
        slot = self.res_dsem[key]
        prev = self.dcnt[slot]
        if prev and self.seen_d[q].get(slot, 0) < prev:
            o.waits.append(Ev("d", slot, prev, {}))
            self.seen_d[q][slot] = prev
        self.dcnt[slot] += 16
        val = self.dcnt[slot]
        o.dma = (slot, out, in_, kw)
        ev = Ev("d", slot, val, dict(self.vc[q]), o)
        for r in writes:
            r.w = ev
            r.r = {}
        for r in reads:
            r.r[("d", slot)] = ev
        return o

    def finish(self, eng, all_res):
        o = Op(eng, len(self.ops[eng]), "nop", "finish")
        self.ops[eng].append(o)
        for r in all_res:
            self._need(o, r.w)
            for ev in r.r.values():
                self._need(o, ev)
        return o

    def emit(self):
        nc = self.nc
        sigcount = {}
        for e in ENGS:
            c = 0
            lst = []
            for o in self.ops[e]:
                if o.signal:
                    c += 1
                lst.append(c)
            sigcount[e] = lst
        engobj = {"pe": "tensor", "act": "scalar", "dve": "vector", "pool": "gpsimd", "sp": "sync"}

        def run(e, eng):
            for o in self.ops[e]:
                for ev in o.waits:
                    if ev.kind == "c":
                        eng.wait_ge(self.sem[ev.eng], sigcount[ev.eng][ev.idx])
                    else:
                        eng.wait_ge(self.dsem[ev.eng], ev.idx)
                if o.dma is not None:
                    slot, out, in_, kw = o.dma
                    eng.dma_start(out=out, in_=in_, **kw).then_inc(self.dsem[slot], 16)
                elif o.fn == "nop":
                    pass
                else:
                    ins = o.fn(eng)
                    if o.signal:
                        ins.then_inc(self.sem[e], 1)

        with nc.Block() as block:
            @block.tensor
            def _(eng):
                run("pe", eng)

            @block.scalar
            def _(eng):
                run("act", eng)

            @block.vector
            def _(eng):
                run("dve", eng)

            @block.gpsimd
            def _(eng):
                run("pool", eng)

            @block.sync
            def _(eng):
                run("sp", eng)


def build(T=T, taps=None):
    NT = T // TT
    taps = taps or ()
    nc = bass.Bass("TRN2", target_bir_lowering=False)
    dr = lambda name, shape, dt=F32, kind="ExternalInput": nc.dram_tensor(name, shape, dt, kind=kind).ap()
    x = dr("x", [T, D])
    p_in = dr("p", [T, PLE])
    attn_pre = dr("attn_pre_norm", [D])
    w_in = dr("w_in", [D, INC])
    hg_gamma = dr("hg_lower_gamma", [2, HGW])
    hg_norm = dr("hg_out_norm", [HGW])
    sb_norm = dr("sb_out_norm", [SBW])
    w_out = dr("w_out", [D, D])
    attn_post = dr("attn_post_norm", [D])
    ffn_pre = dr("ffn_pre_norm", [D])
    w_gu = dr("w_gate_up", [D, 2 * DFF])
    w_down = dr("w_down", [DFF, D])
    ffn_post = dr("ffn_post_norm", [D])
    ple_proj = dr("ple_proj", [PLE, D])
    ple_gate = dr("ple_gate", [D, D])
    y = dr("y", [T, D], F32, "ExternalOutput")
    w_in_b = dr("w_in_b", [D, INC], BF16, "Internal")
    w_out_b = dr("w_out_b", [D, D], BF16, "Internal")
    w_gu_b = dr("w_gu_b", [D, 2 * DFF], BF16, "Internal")
    w_down_b = dr("w_down_b", [DFF, D], BF16, "Internal")
    ple_proj_b = dr("ple_proj_b", [PLE, D], BF16, "Internal")
    ple_gate_b = dr("ple_gate_b", [D, D], BF16, "Internal")

    with contextlib.ExitStack() as st:
        pg = Prog(nc, st)
        sb = lambda name, shape, dt: st.enter_context(nc.sbuf_tensor(name, shape, dt))
        ps = lambda name, shape, dt: st.enter_context(nc.psum_tensor(name, shape, dt))
        R = pg.res

        def ACT(func, out, in_, reads, writes, **kw):
            pg.op("act", lambda e: e.activation(out=out, in_=in_, func=func, **kw), reads=reads, writes=writes)

        def MM(out, lhsT, rhs, start, stop, reads, writes, skip=False):
            pg.op("pe", lambda e: e.matmul(out, lhsT=lhsT, rhs=rhs, start=start, stop=stop, skip_group_check=skip),
                  reads=reads, writes=writes)

        def TT_(eng, out, in0, in1, op, reads, writes):
            pg.op(eng, lambda e: e.tensor_tensor(out=out, in0=in0, in1=in1, op=op), reads=reads, writes=writes)

        def TS(eng, out, in0, s1, s2, op0, op1, reads, writes):
            if s2 is None:
                pg.op(eng, lambda e: e.tensor_scalar(out=out, in0=in0, scalar1=s1, scalar2=None, op0=op0), reads=reads, writes=writes)
            else:
                pg.op(eng, lambda e: e.tensor_scalar(out=out, in0=in0, scalar1=s1, scalar2=s2, op0=op0, op1=op1), reads=reads, writes=writes)

        def STT(eng, out, in0, scalar, in1, op0, op1, reads, writes):
            pg.op(eng, lambda e: e.scalar_tensor_tensor(out=out, in0=in0, scalar=scalar, in1=in1, op0=op0, op1=op1),
                  reads=reads, writes=writes)

        def CP(eng, out, in_, reads, writes):
            if eng == "act":
                pg.op("act", lambda e: e.copy(out=out, in_=in_), reads=reads, writes=writes)
            else:
                pg.op(eng, lambda e: e.tensor_copy(out=out, in_=in_), reads=reads, writes=writes)

        def MEMSET(eng, ap, val, writes, reads=()):
            pg.op(eng, lambda e: e.memset(ap, val), reads=reads, writes=writes)

        tap_outs = []

        def TAP(name, ap, reads, ti=0, only_ti=0):
            if name not in taps or ti != only_ti:
                return
            shape = list(ap.shape)
            d = nc.dram_tensor("tap_" + name, shape, ap.dtype, kind="ExternalOutput").ap()
            rr = R()
            pg.dma("sp", d, ap, reads=reads, writes=[rr])
            tap_outs.append(rr)

        psall = ps("psall", [P, 8 * 512], F32)
        bk = [psall[:, i * 512:(i + 1) * 512] for i in range(8)]
        rb = [R("bk%d" % i, excl=True) for i in range(8)]
        bkT = psall[:, 0:512].bitcast(BF16)
        r_bkT = rb[0]

        ident = sb("ident", [P, P], BF16); r_ident = R()
        ones_bf = sb("ones_bf", [P, P], BF16); r_ones = R()
        zeros_bf = sb("zeros_bf", [P, P], BF16); r_zeros = R()
        tmpf = sb("tmpf", [P, P], F32); r_tmpf = R()
        tri_bf = sb("tri_bf", [P, P], BF16); r_tri = R()
        su_bf = sb("su_bf", [P, P], BF16); r_su = R()
        cmask = sb("cmask", [P, P], BF16); r_cmask = R()
        mh = sb("mh", [P, 4, 64], F32); r_mh = R()
        m1 = sb("m1", [P, 64], F32); m2 = sb("m2", [P, 64], F32); m3 = sb("m3", [P, 64], F32); r_m = R()
        rst = sb("rst", [P, 512], F32); r_rst = R()
        wbc_post = sb("wbc_post", [P, D], F32); r_wbc_post = R()
        wbc_fpost = sb("wbc_fpost", [P, D], F32); r_wbc_fpost = R()
        wbc_pre = sb("wbc_pre", [P, D], F32); r_wbc_pre = R()
        wbc_fpre = sb("wbc_fpre", [P, D], F32); r_wbc_fpre = R()
        rows = sb("rows", [32, P], F32); r_rows = R()
        pv = sb("pv", [P, 32], F32); r_pv = R()
        lbt = sb("lbt", [P, 8], F32); r_lb = R()
        pproj = sb("pproj", [P, 2, D], BF16); r_pproj = R()

        MEMSET("pool", ident[:], 0.0, [r_ident])
        pg.op("pool", lambda e: e.affine_select(out=ident[:], in_=ident[:], compare_op=ALU.not_equal, fill=1.0, base=0,
                                                pattern=[[-1, P]], channel_multiplier=1), reads=[r_ident], writes=[r_ident])
        MEMSET("pool", ones_bf[:], 1.0, [r_ones])
        MEMSET("pool", zeros_bf[:], 0.0, [r_zeros])
        MEMSET("pool", tmpf[:], 1.0, [r_tmpf])
        pg.op("pool", lambda e: e.affine_select(out=tmpf[:], in_=tmpf[:], compare_op=ALU.is_ge, fill=0.0, base=0,
                                                pattern=[[-1, P]], channel_multiplier=1), reads=[r_tmpf], writes=[r_tmpf])
        CP("pool", tri_bf[:], tmpf[:], [r_tmpf], [r_tri])
        MEMSET("pool", tmpf[:], 1.0, [r_tmpf], reads=[r_tmpf])
        pg.op("pool", lambda e: e.affine_select(out=tmpf[:], in_=tmpf[:], compare_op=ALU.is_gt, fill=0.0, base=0,
                                                pattern=[[1, P]], channel_multiplier=-1), reads=[r_tmpf], writes=[r_tmpf])
        CP("pool", su_bf[:], tmpf[:], [r_tmpf], [r_su])
        CP("pool", cmask[:], tmpf[:], [r_tmpf], [r_cmask])
        MEMSET("pool", m1[:], 1.0, [r_m])
        MEMSET("pool", m2[:], 1.0, [r_m])
        MEMSET("pool", m3[:], 1.0, [r_m])
        pg.op("pool", lambda e: e.affine_select(out=m1[:], in_=m1[:], compare_op=ALU.is_ge, fill=0.0, base=0,
                                                pattern=[[1, 64]], channel_multiplier=-1), reads=[r_m], writes=[r_m])
        pg.op("pool", lambda e: e.affine_select(out=m2[:], in_=m2[:], compare_op=ALU.is_ge, fill=0.0, base=64,
                                                pattern=[[1, 64]], channel_multiplier=-1), reads=[r_m], writes=[r_m])
        pg.op("pool", lambda e: e.affine_select(out=m3[:], in_=m3[:], compare_op=ALU.is_ge, fill=0.0, base=-64,
                                                pattern=[[0, 64]], channel_multiplier=1), reads=[r_m], writes=[r_m])
        TT_("pool", m2[:], m2[:], m3[:], ALU.mult, [r_m], [r_m])
        TT_("pool", m1[:], m1[:], m2[:], ALU.add, [r_m], [r_m])
        for m in range(4):
            CP("pool", mh[:, m, :], m1[:], [r_m], [r_mh])
        MEMSET("pool", rst[:], 1.0, [r_rst])
        MEMSET("pool", rst[:].rearrange("p (c t) -> p c t", t=64)[:, :, 0:1], 0.0, [r_rst], reads=[r_rst])

        pg.dma("sp", wbc_post[:], attn_post.partition_broadcast(P), writes=[r_wbc_post])
        pg.dma("sp", wbc_fpost[:], ffn_post.partition_broadcast(P), writes=[r_wbc_fpost])
        pg.dma("sp", wbc_pre[:], attn_pre.partition_broadcast(P), writes=[r_wbc_pre])
        pg.dma("sp", wbc_fpre[:], ffn_pre.partition_broadcast(P), writes=[r_wbc_fpre])
        pg.dma("sp", rows[0:8, :], hg_gamma.rearrange("r (c p) -> (r c) p", p=P), writes=[r_rows])
        r_rows2 = R(); r_rows3 = R(); r_rows4 = R(); r_rows5 = R()
        pg.dma("sp", rows[8:12, :], hg_norm.rearrange("(c p) -> c p", p=P), writes=[r_rows2])
        pg.dma("sp", rows[12:16, :], sb_norm.rearrange("(c p) -> c p", p=P), writes=[r_rows3])
        pg.dma("sp", rows[16:24, :], attn_pre.rearrange("(c p) -> c p", p=P), writes=[r_rows4])
        pg.dma("sp", rows[24:32, :], ffn_pre.rearrange("(c p) -> c p", p=P), writes=[r_rows5])
        identf32 = sb("identf32", [32, 32], F32); r_identf32 = R()
        MEMSET("pool", identf32[:], 0.0, [r_identf32])
        pg.op("pool", lambda e: e.affine_select(out=identf32[:], in_=identf32[:], compare_op=ALU.not_equal, fill=1.0, base=0,
                                                pattern=[[-1, 32]], channel_multiplier=1), reads=[r_identf32], writes=[r_identf32])
        MM(bk[1][:, 0:32], rows[:, :], identf32[:, :], True, True,
           [r_rows, r_rows2, r_rows3, r_rows4, r_rows5, r_identf32], [rb[1]])
        CP("dve", pv[:], bk[1][:, 0:32], [rb[1]], [r_pv])
        TT_("dve", lbt[:, 0:4], pv[:, 0:4], pv[:, 4:8], ALU.subtract, [r_pv], [r_lb])
        ACT(AF.Sigmoid, lbt[:, 0:4], lbt[:, 0:4], [r_lb], [r_lb])
        TS("dve", lbt[:, 4:8], lbt[:, 0:4], -1.0, 1.0, ALU.mult, ALU.add, [r_lb], [r_lb])

        cast_res = {}

        def cast(tag, src, dst, r0, r1, c0, c1):
            rr = R("wb_" + tag)
            w = c1 - c0
            bw = 256 if w % 256 == 0 else 128
            pg.dma("pool", dst[r0:r1, c0:c1].rearrange("r (a b) -> r a b", b=bw), src[r0:r1, c0:c1].rearrange("r (a b) -> r a b", b=bw),
                   writes=[rr])
            cast_res[tag] = [rr]

        IN_ORDER = [("hi", 1024), ("hg", 1536), ("sq", 2048), ("sk", 2560), ("sv", 3072), ("hq", 0), ("hf", 512)]
        GROUPS = [(0, 4), (4, 4), (8, 4), (12, 4), (16, 4), (20, 2)]
        DGRP = [(0, 8), (8, 8), (16, 6)]
        for nm, c0 in IN_ORDER:
            cast("in_" + nm, w_in, w_in_b, 0, D, c0, c0 + 512)
        cast("pproj", ple_proj, ple_proj_b, 0, PLE, 0, D)
        for half in range(2):
            cast("out%d" % half, w_out, w_out_b, 0, D, half * 512, (half + 1) * 512)
        for g, (f0, nf) in enumerate(GROUPS):
            cast("g%d" % g, w_gu, w_gu_b, 0, D, f0 * P, (f0 + nf) * P)
            cast("u%d" % g, w_gu, w_gu_b, 0, D, DFF + f0 * P, DFF + (f0 + nf) * P)
        for half in range(2):
            for fg, (f0, nf) in enumerate(DGRP):
                cast("d%d_%d" % (half, fg), w_down, w_down_b, f0 * P, (f0 + nf) * P, half * 512, (half + 1) * 512)
        for half in range(2):
            cast("pg%d" % half, ple_gate, ple_gate_b, 0, D, half * 512, (half + 1) * 512)

        pg.dma("sp", pproj[:], ple_proj_b.rearrange("(kc p) n -> p kc n", p=P), reads=cast_res["pproj"], writes=[r_pproj])

        NB = T // P
        skT = sb("skT", [P, 4, T], BF16); r_skT = [R() for _ in range(NT)]
        svb = sb("svb", [P, NB, SBW], BF16); r_sv = [R() for _ in range(NT)]
        xt = sb("xt", [P, 4, D], F32); r_xt = [R() for _ in range(4)]
        xn = [sb("xn%d" % i, [P, D], BF16) for i in range(2)]; r_xn = [R(), R()]
        junk = xn[1]; r_junk = r_xn[1]
        ssq = sb("ssq", [P, 8], F32); r_ssq = R()
        rstd = sb("rstd", [P, 8], F32); r_rstd = R()
        uT = sb("uT", [P, 8, TT], BF16); r_uT = [R() for _ in range(8)]
        catH = sb("catH", [P, 4, TT], BF16); r_catH = [R() for _ in range(4)]
        NSL = 3
        hsl = [sb("hsl%d" % i, [P, 8, 512], BF16) for i in range(NSL)]; r_h = [R() for _ in range(NSL)]
        Vt = sb("Vt", [P, 4, HGW], BF16); r_V = R()
        hgT = sb("hgT", [P, 4, TT], BF16); r_hgT = R()
        sqT = sb("sqT", [P, 4, TT], BF16); r_sqT = R()
        carry = sb("carry", [P, 4, P], F32); r_carry = R()
        sqs = sb("sqs", [P, TT], BF16); r_sqs = R()
        sqh = sqs; r_sqh = r_sqs
        catS = sb("catS", [P, 4, TT], BF16); r_catS = [R() for _ in range(4)]
        lnr = sb("lnr", [P, TT], F32); r_lnr = R()
        pt = sb("pt", [P, PLE], F32); r_pt = R()
        pn = sb("pn", [P, PLE], BF16); r_pn = R()
        pT = sb("pT", [P, 2, TT], BF16); r_pT = R()
        small = sb("small", [P, 32], F32); r_small = R()
        small_b = sb("small_b", [P, 24], F32); r_small_b = R()
        smalls = [(small, r_small), (small_b, r_small_b)]
        attn_scr = sb("attn_scr", [P, 6144], BF16)
        e_all = attn_scr[:, 0:2048].rearrange("p (a c t) -> p a c t", a=2, c=2)
        L_all = attn_scr[:, 2048:4096].rearrange("p (a c t) -> p a c t", a=2, c=2)
        X_one = attn_scr[:, 4096:5120].rearrange("p (c t) -> p c t", c=2)
        A_one = attn_scr[:, 5120:6144].rearrange("p (c t) -> p c t", c=2)
        e3 = sb("e3", [P, 2, 512], BF16)
        e_bufs = [e_all[:, 0], e_all[:, 1], e3[:]]
        r_e = [R(), R(), R()]; r_L = [R(), R()]; r_X = R(); r_A = R()
        ybuf = attn_scr[:, 0:4096].bitcast(F32).rearrange("p (j t) -> p j t", j=4); r_ybuf = R()
        tmp = attn_scr[:, 4096:5120].bitcast(F32); r_tmp = R()
        sg = attn_scr[:, 5120:6144].bitcast(F32); r_sg = R()
        actT = sb("actT", [P, 22, TT], BF16); r_actT = R()
        scr_f = actT[:].rearrange("p a b -> p (a b)").bitcast(F32)
        scr_b = actT[:].rearrange("p a b -> p (a b)")
        t1 = scr_f[:, 0:512]; t2 = scr_f[:, 512:1024]; t3 = scr_f[:, 1024:1536]
        qf = scr_f[:, 1536:2048]; ff = scr_f[:, 2048:2560]
        bo = 2560 * 2
        Qt_bf = scr_b[:, bo:bo + 512]; Qp_bf = scr_b[:, bo + 512:bo + 1024]
        Kt_bf = scr_b[:, bo + 1024:bo + 1536]; Kh_bf = scr_b[:, bo + 1536:bo + 2048]
        KhT = scr_b[:, bo + 2048:bo + 2560].rearrange("p (a b) -> p a b", b=P)
        scT = scr_b[:, bo + 2560:bo + 2816]
        Sbf = scr_b[:, bo + 2816:bo + 2816 + 1024].rearrange("p (a b) -> p a b", b=P)
        Sall = scr_f[:, 4480:5632].rearrange("p (a b) -> p a b", b=P); r_Sall = R()
        r_t1 = R(); r_t2 = R(); r_t3 = R(); r_qf = R(); r_ff = R()
        r_Qt = R(); r_Qp = R(); r_Kt = R(); r_Kh = R(); r_KhT = R(); r_scT = R(); r_Sbf = R()
        r_X2 = R()
        pg.alias(r_X2, r_pT)
        X_bufs = [X_one, pT[:]]
        r_Xs = [r_X, r_X2]
        hg_scr = [r_t1, r_t2, r_t3, r_qf, r_ff, r_Qt, r_Qp, r_Kt, r_Kh, r_KhT, r_scT, r_Sbf, r_Sall]
        fence_t = sb("fence_t", [P, 2], F32)
        fence_set = hg_scr + [r_actT, r_ybuf, r_tmp, r_sg, r_X, r_A] + r_e + r_L

        def fence():
            MEMSET("pool", fence_t[:, 0:1], 0.0, fence_set)

        MEMSET("pool", carry[:], 0.0, [r_carry])

        wv_in = w_in_b.rearrange("(kc p) n -> p kc n", p=P)
        wv_out = w_out_b.rearrange("(kc p) n -> p kc n", p=P)
        wv_gu = w_gu_b.rearrange("(kc p) n -> p kc n", p=P)
        wv_dn = w_down_b.rearrange("(kc p) n -> p kc n", p=P)
        wv_pg = ple_gate_b.rearrange("(kc p) n -> p kc n", p=P)
        def tile_blocks():
            bl = []
            for nm, c0 in IN_ORDER:
                bl.append(("in_" + nm, wv_in[:, :, c0:c0 + 512], 8, 512, cast_res["in_" + nm]))
            for half in range(2):
                bl.append(("out%d" % half, wv_out[:, :, half * 512:(half + 1) * 512], 8, 512, cast_res["out%d" % half]))
            for g, (f0, nf) in enumerate(GROUPS):
                bl.append(("g%d" % g, wv_gu[:, :, f0 * P:(f0 + nf) * P], 8, nf * P, cast_res["g%d" % g]))
                bl.append(("u%d" % g, wv_gu[:, :, DFF + f0 * P:DFF + (f0 + nf) * P], 8, nf * P, cast_res["u%d" % g]))
            for half in range(2):
                for fg, (f0, nf) in enumerate(DGRP):
                    bl.append(("d%d_%d" % (half, fg), wv_dn[:, f0:f0 + nf, half * 512:(half + 1) * 512], nf, 512, cast_res["d%d_%d" % (half, fg)]))
            for half in range(2):
                bl.append(("pg%d" % half, wv_pg[:, :, half * 512:(half + 1) * 512], 8, 512, cast_res["pg%d" % half]))
            return bl

        blocks = []
        for ti in range(NT):
            blocks.extend(tile_blocks())
        wstate = {"issued": 0, "next": 0, "released": 0}

        def _issue_upto(n):
            while wstate["issued"] < min(len(blocks), n):
                j = wstate["issued"]
                assert j - NSL < wstate["released"], "slot still live"
                _, view, nk, ncol, rsrc = blocks[j]
                pg.dma("sp", hsl[j % NSL][:, 0:nk, 0:ncol], view, reads=rsrc, writes=[r_h[j % NSL]])
                wstate["issued"] += 1

        def wnext(tag):
            i = wstate["next"]
            wstate["next"] += 1
            assert blocks[i][0] == tag, (blocks[i][0], tag)
            _issue_upto(i + 1)
            return hsl[i % NSL], r_h[i % NSL]

        def wrel(k=1):
            wstate["released"] += k
            assert wstate["released"] <= wstate["next"]
            _issue_upto(wstate["released"] + NSL)

        bank_rot = {"i": 0}

        def nextbank(choices):
            b = choices[bank_rot["i"] % len(choices)]
            bank_rot["i"] += 1
            return b

        def fm_mm(b, W, rW, col0, ncols=P):
            for k in range(8):
                MM(bk[b][0:ncols, :], W[:, k, col0:col0 + ncols], uT[:, k, :], k == 0, k == 7, [rW, r_uT[k]], [rb[b]])

        def tm_mm(b, j, W, rW, col0, ncols=512):
            for k in range(8):
                MM(bk[b][:, 0:ncols], uT[:, k, j * P:(j + 1) * P], W[:, k, col0:col0 + ncols], k == 0, k == 7,
                   [rW, r_uT[k]], [rb[b]])

        bkT2 = psall[:, 7 * 512:8 * 512].bitcast(BF16)
        tr_banks = [(bkT, rb[0]), (bkT2, rb[7])]

        def norm_transpose(wbc, r_wbc, src_is_norm=True):
            if src_is_norm:
                for j in range(4):
                    ACT(AF.Square, junk[:], xt[:, j, :], [r_xt[j]], [r_junk, r_ssq], accum_out=ssq[:, j:j + 1])
                    ACT(AF.Ln, rstd[:, j:j + 1], ssq[:, j:j + 1], [r_ssq], [r_rstd], scale=1.0 / D, bias=EPS)
                    ACT(AF.Exp, rstd[:, j:j + 1], rstd[:, j:j + 1], [r_rstd], [r_rstd], scale=-0.5)
            for j in range(4):
                xb = j % 2
                tb, rtb = tr_banks[j % 2]
                if src_is_norm:
                    STT("dve", xn[xb][:], xt[:, j, :], rstd[:, j:j + 1], wbc[:], ALU.mult, ALU.mult, [r_xt[j], r_rstd, r_wbc], [r_xn[xb]])
                else:
                    CP("act" if j % 2 == 0 else "dve", xn[xb][:], xt[:, j, :], [r_xt[j]], [r_xn[xb]])
                for k in range(8):
                    pg.op("pe", lambda e, k=k, xb=xb, tb=tb: e.transpose(out=tb[:, k * P:(k + 1) * P], in_=xn[xb][:, k * P:(k + 1) * P],
                                                                          identity=ident[:]), reads=[r_xn[xb], r_ident], writes=[rtb])
                CP("act" if j % 2 == 0 else "dve", uT[:, :, j * P:(j + 1) * P], tb.rearrange("p (k t) -> p k t", k=8), [rtb], r_uT)

        acc_hg = xn[0][:].bitcast(F32)
        acc_sb = xn[1][:].bitcast(F32)

        def bc_rstd(acc, r_acc, width, buf, rbuf):
            ACT(AF.Ln, lnr[:], acc, [r_acc], [r_lnr], scale=1.0 / width, bias=EPS)
            ACT(AF.Exp, lnr[:], lnr[:], [r_lnr], [r_lnr], scale=-0.5)
            for c in range(4):
                TT_("dve", buf[:, c, :], buf[:, c, :], lnr[:], ALU.mult, [rbuf[c], r_lnr], [rbuf[c]])

        for ti in range(NT):
            tok0 = ti * TT
            if ti == 0:
                for j in range(4):
                    pg.dma("sp", xt[:, j, :], x[tok0 + j * P:tok0 + (j + 1) * P, :], writes=[r_xt[j]])
            norm_transpose(wbc_pre, r_wbc_pre)
            TAP('uT1', uT[:], r_uT, ti)
            fence()
            W, rW = wnext("in_hi")
            for j in range(4):
                b = nextbank([1, 2])
                tm_mm(b, j, W, rW, 0)
                CP("act", Vt[:, j, :], bk[b][:], [rb[b]], [r_V])
            wrel()
            W, rW = wnext("in_hg")
            for c in range(4):
                b = nextbank([1, 2])
                fm_mm(b, W, rW, c * P)
                ACT(AF.Silu, hgT[:, c, :], bk[b][:], [rb[b]], [r_hgT])
            wrel()
            W, rW = wnext("in_sq")
            for c in range(4):
                b = nextbank([1, 2])
                fm_mm(b, W, rW, c * P)
                CP("dve", sqT[:, c, :], bk[b][:], [rb[b]], [r_sqT])
            wrel()
            W, rW = wnext("in_sk")
            for c in range(4):
                b = nextbank([1, 2])
                fm_mm(b, W, rW, c * P)
                CP("dve", skT[:, c, tok0:tok0 + TT], bk[b][:], [rb[b]], [r_skT[ti]])
            wrel()
            W, rW = wnext("in_sv")
            for j in range(4):
                b = nextbank([1, 2])
                tm_mm(b, j, W, rW, 0)
                CP("act", svb[:, 4 * ti + j, :], bk[b][:], [rb[b]], [r_sv[ti]])
            wrel()
            TAP('Vt', Vt[:], [r_V], ti); TAP('hgT', hgT[:], [r_hgT], ti); TAP('sqT', sqT[:], [r_sqT], ti)
            TAP('skT', skT[:, :, 0:512], [r_skT[0]], ti); TAP('sv', svb[:, 0:4, :], [r_sv[0]], ti)
            Wq, rWq = wnext("in_hq")
            Wf, rWf = wnext("in_hf")
            b3 = lambda ap: ap.rearrange("p (c t) -> p c t", t=64)
            HQ, HF, HX = 2, 3, 7
            bkX = psall[:, HX * 512:(HX + 1) * 512].bitcast(BF16)

            HINT = [2.5, 2.5, 3.0, 3.0, 1.5, 2.0, 2.0, 2.0, 2.0, 1.5, 1.0, 2.0, 2.0, 1.0, 1.5, 2.0, 1.5, 2.5, 1.5, 1.0, 1.0]

            def hg_gen():
                for h in range(4):
                    sm, r_sm = smalls[h % 2]
                    for k in range(8):
                        MM(bk[HQ][:, :], Wq[:, k, h * P:(h + 1) * P], uT[:, k, :], k == 0, k == 7, [rWq, r_uT[k]], [rb[HQ]])
                        if k == 3:
                            yield 1.3
                    yield HINT[0]
                    for k in range(8):
                        MM(bk[HF][:, :], Wf[:, k, h * P:(h + 1) * P], uT[:, k, :], k == 0, k == 7, [rWf, r_uT[k]], [rb[HF]])
                        if k == 3:
                            ACT(AF.Exp, t3, bk[HQ][:], [rb[HQ]], [r_t3], scale=-1.0)
                            yield 1.3
                    yield HINT[1]
                    ACT(AF.Exp, t1, bk[HF][:], [rb[HF]], [r_t1], scale=-1.0)
                    TS("dve", t3, t3, 1.0, None, ALU.add, None, [r_t3], [r_t3])
                    pg.op("dve", lambda e: e.reciprocal(out=t3, in_=t3), reads=[r_t3], writes=[r_t3])
                    TT_("dve", qf, bk[HQ][:], t3, ALU.mult, [rb[HQ], r_t3], [r_qf])
                    yield HINT[2]
                    TS("dve", t1, t1, 1.0, None, ALU.add, None, [r_t1], [r_t1])
                    pg.op("dve", lambda e: e.reciprocal(out=t1, in_=t1), reads=[r_t1], writes=[r_t1])
                    TS("dve", ff, t1, lbt[:, 4 + h:5 + h], lbt[:, h:h + 1], ALU.mult, ALU.add, [r_t1, r_lb], [r_ff])
                    yield HINT[3]
                    yield 3.0
                    ACT(AF.Ln, t1, ff, [r_ff], [r_t1])
                    TS("pool", ff, ff, -1.0, 1.0, ALU.mult, ALU.add, [r_ff], [r_ff])
                    yield HINT[4]
                    pg.op("dve", lambda e: e.tensor_tensor_scan(out=t2, data0=rst[:], data1=t1, initial=0.0, op0=ALU.mult, op1=ALU.add),
                          reads=[r_rst, r_t1], writes=[r_t2])
                    TT_("dve", b3(t1), b3(t2), b3(t2)[:, :, 31:32].broadcast_to([P, 8, 64]), ALU.subtract, [r_t2], [r_t1])
                    yield HINT[5]
                    yield 2.5
                    ACT(AF.Exp, sm[:, 0:8], b3(t2)[:, :, 31], [r_t2], [r_sm])
                    ACT(AF.Exp, t3, t1, [r_t1], [r_t3])
                    yield HINT[6]
                    ACT(AF.Exp, t1, t1, [r_t1], [r_t1], scale=-1.0)
                    CP("pool", sm[:, 8:16], b3(t3)[:, :, 63], [r_t3], [r_sm])
                    TT_("pool", sm[:, 16:24], sm[:, 8:16], sm[:, 0:8], ALU.mult, [r_sm], [r_sm])
                    TT_("dve", qf, qf, t3, ALU.mult, [r_qf, r_t3], [r_qf])
                    yield HINT[7]
                    TT_("pool", ff, ff, t1, ALU.mult, [r_ff, r_t1], [r_ff])
                    CP("pool", Qt_bf, qf, [r_qf], [r_Qt])
                    TT_("dve", b3(Qp_bf), b3(qf), sm[:, 0:8].unsqueeze(2).broadcast_to([P, 8, 64]), ALU.mult, [r_qf, r_sm], [r_Qp])
                    yield HINT[8]
                    CP("pool", Kt_bf, ff, [r_ff], [r_Kt])
                    TT_("dve", b3(Kh_bf), b3(ff), sm[:, 8:16].unsqueeze(2).broadcast_to([P, 8, 64]), ALU.mult, [r_ff, r_sm], [r_Kh])
                    yield HINT[9]
                    yield HINT[10]
                    for blk in range(4):
                        pg.op("pe", lambda e, blk=blk: e.transpose(out=bkX[:, blk * P:(blk + 1) * P], in_=Kh_bf[:, blk * P:(blk + 1) * P],
                                                                    identity=ident[:]), reads=[r_Kh, r_ident], writes=[rb[HX]])
                    yield 1.0
                    for c in range(8):
                        m, pr = c // 2, 64 * (c % 2)
                        MM(bk[HQ][pr:pr + 64, m * 64:(m + 1) * 64], Kt_bf[:, c * 64:(c + 1) * 64], Qt_bf[:, c * 64:(c + 1) * 64],
                           True, True, [r_Kt, r_Qt], [rb[HQ]])
                    yield HINT[11]
                    CP("dve", KhT, bkX[:, 0:512].rearrange("p (a b) -> p a b", b=P), [rb[HX]], [r_KhT])
                    TT_("dve", scT, bk[HQ][:, 0:256], mh[:].rearrange("p a b -> p (a b)"), ALU.mult, [rb[HQ], r_mh], [r_scT])
                    CP("pool", Sall[:, 0, :], carry[:, h, :], [r_carry], [r_Sall])
                    yield HINT[12]
                    yield HINT[13]
                    for c in range(8):
                        m, pr = c // 2, 64 * (c % 2)
                        ub = HF if c % 2 == 0 else HX
                        MM(bk[ub][:, m * P:(m + 1) * P], KhT[pr:pr + 64, m, :], Vt[pr:pr + 64, m, h * P:(h + 1) * P],
                           True, True, [r_KhT, r_V], [rb[ub]])
                    yield HINT[14]
                    for c in range(8):
                        m = c // 2
                        ub = HF if c % 2 == 0 else HX
                        STT("dve", Sall[:, c + 1, :], Sall[:, c, :], sm[:, 16 + c:17 + c], bk[ub][:, m * P:(m + 1) * P],
                            ALU.mult, ALU.add, [r_Sall, r_sm, rb[ub]], [r_Sall])
                        if c % 2 == 1:
                            yield 2.5
                    CP("pool", carry[:, h, :], Sall[:, 8, :], [r_Sall], [r_carry])
                    CP("pool", Sbf, Sall[:, 0:8, :], [r_Sall], [r_Sbf])
                    yield HINT[15]
                    yield HINT[16]
                    for c in range(8):
                        m, pr = c // 2, 64 * (c % 2)
                        MM(bk[HQ][:, c * 64:(c + 1) * 64], Sbf[:, c, :], Qp_bf[:, c * 64:(c + 1) * 64], True, False,
                           [r_Sbf, r_Qp], [rb[HQ]])
                        MM(bk[HQ][:, c * 64:(c + 1) * 64], Vt[pr:pr + 64, m, h * P:(h + 1) * P], scT[pr:pr + 64, m * 64:(m + 1) * 64],
                           False, True, [r_V, r_scT], [rb[HQ]])
                        if c == 3:
                            yield 1.0
                    yield HINT[17]
                    STT("dve", catH[:, h, :], bk[HQ][:], pv[:, 8 + h:9 + h], hgT[:, h, :], ALU.mult, ALU.mult,
                        [rb[HQ], r_pv, r_hgT], [r_catH[h]])
                    ACT(AF.Square, sqh[:], bk[HQ][:], [rb[HQ]], [r_sqh])
                    yield HINT[18]
                    MM(bk[HF][:], ones_bf[:], sqh[:], True, True, [r_ones, r_sqh], [rb[HF]])
                    yield HINT[19]
                    if h == 0:
                        CP("dve", acc_hg, bk[HF][:], [rb[HF]], [r_xn[0]])
                    else:
                        TT_("dve", acc_hg, acc_hg, bk[HF][:], ALU.add, [rb[HF], r_xn[0]], [r_xn[0]])
                    yield HINT[20]

            hgen = hg_gen()
            hg_state = {"done": False, "budget": 0.0, "need": 0.0}

            def hg_advance(dt):
                hg_state["budget"] += dt
                while not hg_state["done"] and hg_state["budget"] >= hg_state["need"]:
                    hg_state["budget"] -= hg_state["need"]
                    try:
                        hg_state["need"] = next(hgen)
                    except StopIteration:
                        hg_state["done"] = True

            kbs = list(range(4 * ti + 3, -1, -1))
            nst = len(kbs)
            SLOT_US = 3.3
            HG_TOTAL = 4 * (sum(HINT) + 4 * 2.5 + 1.3 * 2 + 2.0 + 5.5)
            per_slot = SLOT_US * max(1.0, HG_TOTAL / (4 * nst * SLOT_US))
            psC2 = psall[:, 4 * 512:6 * 512].rearrange("p (c t) -> p c t", c=2)
            psZ2 = psall[:, 0:1024].rearrange("p (c t) -> p c t", c=2)

            def t0_of(k):
                return max(0, kbs[k] - 4 * ti) * P

            NG = 4 * nst

            def Zs(g):
                pp, k = divmod(g, nst)
                kb, t0 = kbs[k], t0_of(k)
                for hh in range(2):
                    pb = 64 * hh
                    MM(bk[hh][:, t0:512], skT[pb:pb + 64, pp, kb * P:(kb + 1) * P], sqT[pb:pb + 64, pp, t0:512], True, True,
                       [r_skT[kb // 4], r_sqT], [rb[hh]])

            def Es(g):
                pp, k = divmod(g, nst)
                kb, pe3, t0 = kbs[k], g % 3, t0_of(k)
                eb = e_bufs[pe3]
                ACT(AF.Exp, eb[:, :, t0:512], psZ2[:, :, t0:512], [rb[0], rb[1]], [r_e[pe3]], scale=0.125)
                if kb >= 4 * ti:
                    TT_("pool", eb[:, :, t0:t0 + P], eb[:, :, t0:t0 + P],
                        cmask[:].unsqueeze(1).broadcast_to([P, 2, P]), ALU.mult, [r_e[pe3], r_cmask], [r_e[pe3]])

            def Ls(g):
                k = g % nst
                par, t0 = g % 2, t0_of(k)
                ACT(AF.Ln, L_all[:, par, :, t0:512], e_bufs[g % 3][:, :, t0:512], [r_e[g % 3]], [r_L[par]], bias=1.0)

            def Ts(g):
                k = g % nst
                par, t0 = g % 2, t0_of(k)
                if k == 0:
                    for _z in range(4):
                        MM(bk[4][:, _z * P:(_z + 1) * P], zeros_bf[:], zeros_bf[:], _z == 0, False, [r_zeros], [rb[4]], skip=True)
                    for _z in range(4):
                        MM(bk[5][:, _z * P:(_z + 1) * P], zeros_bf[:], zeros_bf[:], _z == 0, False, [r_zeros], [rb[5]], skip=True)
                for hh in range(2):
                    MM(bk[4 + hh][:, t0:512], tri_bf[:], L_all[:, par, hh, t0:512], False, False, [r_tri, r_L[par]], [rb[4 + hh]], skip=True)

            def Xs(g):
                k = g % nst
                t0 = t0_of(k)
                ACT(AF.Exp, X_bufs[g % 2][:, :, t0:512], psC2[:, :, t0:512], [rb[4], rb[5]], [r_Xs[g % 2]], scale=-1.0)

            def Ss(g):
                k = g % nst
                par, t0 = g % 2, t0_of(k)
                if k == nst - 1:
                    return
                for hh in range(2):
                    MM(bk[4 + hh][:, t0:512], su_bf[:], L_all[:, par, hh, t0:512], False, False, [r_su, r_L[par]], [rb[4 + hh]], skip=True)

            def As(g):
                k = g % nst
                par, t0 = g % 2, t0_of(k)
                TT_("dve", A_one[:, :, t0:512], e_bufs[g % 3][:, :, t0:512], X_bufs[g % 2][:, :, t0:512], ALU.mult,
                    [r_e[g % 3], r_Xs[g % 2]], [r_A])

            def Vs(g):
                pp, k = divmod(g, nst)
                kb, t0 = kbs[k], t0_of(k)
                if k == 0:
                    for _z in range(4):
                        MM(bk[6][:, _z * P:(_z + 1) * P], zeros_bf[:], zeros_bf[:], _z == 0, False, [r_zeros], [rb[6]], skip=True)
                for hh in range(2):
                    hd = 2 * pp + hh
                    pb = 64 * hh
                    MM(bk[6][pb:pb + 64, t0:512], svb[:, kb, hd * 64:(hd + 1) * 64], A_one[:, hh, t0:512], False, False,
                       [r_sv[kb // 4], r_A], [rb[6]], skip=True)
                if k == nst - 1:
                    TS("dve", catS[:, pp, :], bk[6][:], pv[:, 12 + pp:13 + pp], None, ALU.mult, None, [rb[6], r_pv], [r_catS[pp]])
                    ACT(AF.Square, sqs[:], bk[6][:], [rb[6]], [r_sqs])
                    MM(bk[6][:], ones_bf[:], sqs[:], True, True, [r_ones, r_sqs], [rb[6]])
                    if pp == 0:
                        CP("dve", acc_sb, bk[6][:], [rb[6]], [r_xn[1]])
                    else:
                        TT_("dve", acc_sb, acc_sb, bk[6][:], ALU.add, [rb[6], r_xn[1]], [r_xn[1]])

            Zs(0)
            Es(0)
            Ls(0)
            if NG > 1:
                Zs(1)
            for g in range(NG):
                Ts(g)
                if g >= 1:
                    Vs(g - 1)
                if g + 1 < NG:
                    Es(g + 1)
                if g + 2 < NG:
                    Zs(g + 2)
                hg_advance(per_slot)
                Xs(g)
                if g + 1 < NG:
                    Ls(g + 1)
                Ss(g)
                As(g)
            Vs(NG - 1)
            hg_advance(10 ** 9)
            assert hg_state["done"]
            wrel(2)
            TAP('cat_hg_raw', catH[:], r_catH, ti)
            bc_rstd(acc_hg, r_xn[0], HGW, catH, r_catH)
            TAP('cat_hg', catH[:], r_catH, ti)
            TAP('cat_sb_raw', catS[:], r_catS, ti)
            bc_rstd(acc_sb, r_xn[1], SBW, catS, r_catS)
            TAP('cat_sb', catS[:], r_catS, ti)
            fence()
            W0, rW0 = wnext("out0")
            W1, rW1 = wnext("out1")
            for j in range(4):
                b0, b1 = (1, 2) if j % 2 == 0 else (3, 4)
                for half, (bb, W, rW) in enumerate(((b0, W0, rW0), (b1, W1, rW1))):
                    for k in range(8):
                        src, rsrc = (catH[:, k, j * P:(j + 1) * P], r_catH[k]) if k < 4 else (catS[:, k - 4, j * P:(j + 1) * P], r_catS[k - 4])
                        MM(bk[bb][:], src, W[:, k, :], k == 0, k == 7, [rW, rsrc], [rb[bb]])
                    ACT(AF.Square, junk[:, 0:512], bk[bb][:], [rb[bb]], [r_junk, r_ssq], accum_out=ssq[:, 4 + half:5 + half])
                TT_("dve", ssq[:, 6:7], ssq[:, 4:5], ssq[:, 5:6], ALU.add, [r_ssq], [r_ssq])
                ACT(AF.Ln, rstd[:, 4:5], ssq[:, 6:7], [r_ssq], [r_rstd], scale=1.0 / D, bias=EPS)
                ACT(AF.Exp, rstd[:, 4:5], rstd[:, 4:5], [r_rstd], [r_rstd], scale=-0.5)
                for half, bb in enumerate((b0, b1)):
                    tb_, rtb_ = (tmp, r_tmp) if half == 0 else (lnr[:], r_lnr)
                    STT("dve", tb_, bk[bb][:], rstd[:, 4:5], wbc_post[:, half * 512:(half + 1) * 512], ALU.mult, ALU.mult,
                        [rb[bb], r_rstd, r_wbc_post], [rtb_])
                    TT_("pool", xt[:, j, half * 512:(half + 1) * 512], xt[:, j, half * 512:(half + 1) * 512], tb_, ALU.add,
                        [rtb_, r_xt[j]], [r_xt[j]])
            wrel(2)
            TAP('h1', xt[:], r_xt, ti)
            norm_transpose(wbc_fpre, r_wbc_fpre)
            fence()
            gi = 0
            for g, (f0, nf) in enumerate(GROUPS):
                Wg, rWg = wnext("g%d" % g)
                Wu, rWu = wnext("u%d" % g)
                for c in range(nf):
                    bg, bu = (1, 2) if gi % 2 == 0 else (3, 4)
                    gi += 1
                    fm_mm(bg, Wg, rWg, c * P)
                    fm_mm(bu, Wu, rWu, c * P)
                    ACT(AF.Silu, sg, bk[bg][:], [rb[bg]], [r_sg])
                    TT_("dve", actT[:, f0 + c, :], bk[bu][:], sg, ALU.mult, [rb[bu], r_sg], [r_actT])
                wrel(2)
            for fg, (f0, nf) in enumerate(DGRP):
                W, rW = wnext("d0_%d" % fg)
                for j in range(4):
                    for c in range(nf):
                        fc = f0 + c
                        MM(bk[1 + j][:], actT[:, fc, j * P:(j + 1) * P], W[:, c, :], fc == 0, fc == 21, [rW, r_actT], [rb[1 + j]])
                wrel()
            for j in range(4):
                ACT(AF.Square, junk[:, 0:512], bk[1 + j][:], [rb[1 + j]], [r_junk, r_small], accum_out=small[:, 24 + j:25 + j])
                CP("dve", ybuf[:, j, :], bk[1 + j][:], [rb[1 + j]], [r_ybuf])
            Wd = [wnext("d1_%d" % fg) for fg in range(3)]
            for j in range(4):
                for fg, (f0, nf) in enumerate(DGRP):
                    W, rW = Wd[fg]
                    for c in range(nf):
                        fc = f0 + c
                        MM(bk[1 + j][:], actT[:, fc, j * P:(j + 1) * P], W[:, c, :], fc == 0, fc == 21, [rW, r_actT], [rb[1 + j]])
                ACT(AF.Square, junk[:, 0:512], bk[1 + j][:], [rb[1 + j]], [r_junk, r_small], accum_out=small[:, 28 + j:29 + j])
                TT_("dve", ssq[:, j:j + 1], small[:, 24 + j:25 + j], small[:, 28 + j:29 + j], ALU.add, [r_small], [r_ssq])
                ACT(AF.Ln, rstd[:, j:j + 1], ssq[:, j:j + 1], [r_ssq], [r_rstd], scale=1.0 / D, bias=EPS)
                ACT(AF.Exp, rstd[:, j:j + 1], rstd[:, j:j + 1], [r_rstd], [r_rstd], scale=-0.5)
                STT("dve", tmp, ybuf[:, j, :], rstd[:, j:j + 1], wbc_fpost[:, 0:512], ALU.mult, ALU.mult,
                    [r_ybuf, r_rstd, r_wbc_fpost], [r_tmp])
                TT_("pool", xt[:, j, 0:512], xt[:, j, 0:512], tmp, ALU.add, [r_tmp, r_xt[j]], [r_xt[j]])
                STT("dve", lnr[:], bk[1 + j][:], rstd[:, j:j + 1], wbc_fpost[:, 512:1024], ALU.mult, ALU.mult,
                    [rb[1 + j], r_rstd, r_wbc_fpost], [r_lnr])
                TT_("pool", xt[:, j, 512:1024], xt[:, j, 512:1024], lnr[:], ALU.add, [r_lnr, r_xt[j]], [r_xt[j]])
            wrel(3)
            TAP('h2', xt[:], r_xt, ti)
            TAP('actT', actT[:], [r_actT], ti)
            norm_transpose(None, None, src_is_norm=False)
            for j in range(4):
                pg.dma("sp", pt[:], p_in[tok0 + j * P:tok0 + (j + 1) * P, :], writes=[r_pt])
                CP("dve", pn[:], pt[:], [r_pt], [r_pn])
                for k in range(2):
                    pg.op("pe", lambda e, k=k: e.transpose(out=bkT[:, k * P:(k + 1) * P], in_=pn[:, k * P:(k + 1) * P],
                                                            identity=ident[:]), reads=[r_pn, r_ident], writes=[r_bkT])
                CP("act", pT[:, :, j * P:(j + 1) * P], bkT[:, 0:256].rearrange("p (a b) -> p a b", b=P), [r_bkT], [r_pT])
            Wg0, rWg0 = wnext("pg0")
            Wg1, rWg1 = wnext("pg1")
            for j in range(4):
                for half, (W, rW) in enumerate(((Wg0, rWg0), (Wg1, rWg1))):
                    bg, bp = (1, 2) if (2 * j + half) % 2 == 0 else (3, 4)
                    tm_mm(bg, j, W, rW, 0)
                    for k in range(2):
                        MM(bk[bp][:], pT[:, k, j * P:(j + 1) * P], pproj[:, k, half * 512:(half + 1) * 512], k == 0, k == 1,
                           [r_pT, r_pproj], [rb[bp]])
                    ACT(AF.Sigmoid, sg, bk[bg][:], [rb[bg]], [r_sg])
                    tb_, rtb_ = (tmp, r_tmp) if half == 0 else (lnr[:], r_lnr)
                    TT_("dve", tb_, bk[bp][:], sg, ALU.mult, [rb[bp], r_sg], [rtb_])
                    TT_("pool", tb_, tb_, xt[:, j, half * 512:(half + 1) * 512], ALU.add, [rtb_, r_xt[j]], [rtb_])
                    pg.dma("sp", y[tok0 + j * P:tok0 + (j + 1) * P, half * 512:(half + 1) * 512], tb_, reads=[rtb_])
                if ti + 1 < NT:
                    pg.dma("sp", xt[:, j, :], x[tok0 + TT + j * P:tok0 + TT + (j + 1) * P, :], writes=[r_xt[j]])
            wrel(2)
        pg.finish("sp", r_xt + [r_tmp, r_lnr] + tap_outs)
        if _os.environ.get('SBUF_REPORT'):
            print('sbuf remaining', nc.sbuf_bytes_remaining, {e: len(v) for e, v in pg.ops.items()})
        pg.emit()
    return nc


_NC_CACHE = {}


def kernel(**inputs):
    x = np.asarray(inputs["x"], dtype=np.float32)
    B, Tn, _ = x.shape
    if Tn not in _NC_CACHE:
        _NC_CACHE[Tn] = build(Tn)
    nc = _NC_CACHE[Tn]
    shared = {}
    for name in ("attn_pre_norm", "w_in", "hg_out_norm", "sb_out_norm", "w_out", "attn_post_norm", "ffn_pre_norm",
                 "w_gate_up", "w_down", "ffn_post_norm", "ple_proj", "ple_gate"):
        a = np.asarray(inputs[name], dtype=np.float32)
        shared[name] = np.ascontiguousarray(a[0])
    shared["hg_lower_gamma"] = np.ascontiguousarray(np.asarray(inputs["hg_lower_gamma"], dtype=np.float32))
    p = np.asarray(inputs["p"], dtype=np.float32)[0]
    in_maps = []
    for b in range(B):
        m = dict(shared)
        m["x"] = np.ascontiguousarray(x[b])
        m["p"] = np.ascontiguousarray(p[b])
        in_maps.append(m)
    res = run_bass_kernel_spmd(nc, in_maps, core_ids=list(range(B)))
    return np.stack([np.asarray(r["y"]) for r in res.results], axis=0).astype(np.float32)
```

```python
import contextlib
import os as _os
_DBGH = int(_os.environ.get('DBGH', '0'))
import numpy as np
import concourse.bass as bass
import concourse.mybir as mybir
from concourse.bass_utils import run_bass_kernel_spmd

F32 = mybir.dt.float32
BF16 = mybir.dt.bfloat16
AF = mybir.ActivationFunctionType
ALU = mybir.AluOpType
AX = mybir.AxisListType

D = 1024
T = 4096
PLE = 256
HGW = 512
SBW = 512
DFF = 2816
INC = 3584
EPS = 1e-6
TT = 512
NTT = T // TT
P = 128
NCORES = 8

ENGS = ("pe", "act", "dve", "pool", "sp")


class Res:
    __slots__ = ("name", "w", "r", "alias", "excl")

    def __init__(self, name, excl=False):
        self.name = name
        self.w = None
        self.r = {}
        self.alias = []
        self.excl = excl


class Ev:
    __slots__ = ("kind", "eng", "idx", "vc", "op")

    def __init__(self, kind, eng, idx, vc, op=None):
        self.kind = kind
        self.eng = eng
        self.idx = idx
        self.vc = vc
        self.op = op


class Op:
    __slots__ = ("eng", "idx", "fn", "waits", "signal", "dma", "name")

    def __init__(self, eng, idx, fn, name=""):
        self.eng = eng
        self.idx = idx
        self.fn = fn
        self.waits = []
        self.signal = False
        self.dma = None
        self.name = name


class Prog:
    def __init__(self, nc, stack, n_dma_sems=60, n_pool_sems=12):
        self.n_pool_sems = n_pool_sems
        self.next_dsem_q = {}
        self.nc = nc
        self.stack = stack
        self.ops = {e: [] for e in ENGS}
        self.vc = {e: {f: 0 for f in ENGS} for e in ENGS}
        self.seen_d = {e: {} for e in ENGS}
        self.sem = {e: stack.enter_context(nc.semaphore("s_" + e)) for e in ENGS}
        self.dsem = [stack.enter_context(nc.semaphore("d%d" % i)) for i in range(n_dma_sems)]
        self.dcnt = [0] * n_dma_sems
        self.res_dsem = {}
        self.next_dsem = 0
        self.nres = 0

    def res(self, name=None, excl=False):
        self.nres += 1
        return Res(name or ("r%d" % self.nres), excl)

    def alias(self, a, b):
        a.alias.append(b)
        b.alias.append(a)

    def _need(self, op, ev):
        e = op.eng
        if ev is None:
            return
        if ev.kind == "c":
            if ev.eng == e and e in ("pe", "sp"):
                return
            if self.vc[e][ev.eng] >= ev.idx + 1:
                return
            op.waits.append(ev)
            ev.op.signal = True
            self.vc[e][ev.eng] = ev.idx + 1
            for f, v in ev.vc.items():
                if self.vc[e][f] < v:
                    self.vc[e][f] = v
        else:
            slot, val = ev.eng, ev.idx
            if self.seen_d[e].get(slot, 0) >= val:
                return
            op.waits.append(ev)
            self.seen_d[e][slot] = val
            for f, v in ev.vc.items():
                if self.vc[e][f] < v:
                    self.vc[e][f] = v

    def _expand(self, lst):
        out = []
        for r in lst:
            out.append(r)
            out.extend(r.alias)
        return out

    def _deps(self, op, reads, writes):
        for r in reads:
            self._need(op, r.w)
        for r in writes:
            self._need(op, r.w)
            for ev in r.r.values():
                self._need(op, ev)

    def op(self, eng, fn, reads=(), writes=(), name=""):
        reads = self._expand(reads)
        writes = self._expand(writes)
        ex = [r for r in reads if r.excl]
        if ex:
            writes = writes + [r for r in ex if r not in writes]
            reads = [r for r in reads if not r.excl]
        lst = self.ops[eng]
        o = Op(eng, len(lst), fn, name)
        lst.append(o)
        self._deps(o, reads, writes)
        ev = Ev("c", eng, o.idx, dict(self.vc[eng]), o)
        for r in writes:
            r.w = ev
            r.r = {}
        for r in reads:
            r.r[eng] = ev
        return o

    def dma(self, q, out, in_, reads=(), writes=(), name="", **kw):
        reads = self._expand(reads)
        writes = self._expand(writes)
        lst = self.ops[q]
        o = Op(q, len(lst), None, name)
        lst.append(o)
        self._deps(o, reads, writes)
        key = (q, writes[0] if writes else reads[0])
        if key not in self.res_dsem:
            lo, n = (0, self.n_pool_sems) if q == "pool" else (self.n_pool_sems, len(self.dsem) - self.n_pool_sems)
            self.res_dsem[key] = lo + self.next_dsem_q.get(q != "pool", 0) % n
            self.next_dsem_q[q != "pool"] = self.next_dsem_q.get(q != "pool", 0) + 1
        slot = self.res_dsem[key]
        prev = self.dcnt[slot]
        if prev and self.seen_d[q].get(slot, 0) < prev:
            o.waits.append(Ev("d", slot, prev, {}))
            self.seen_d[q][slot] = prev
        self.dcnt[slot] += 16
        val = self.dcnt[slot]
        o.dma = (slot, out, in_, kw)
        ev = Ev("d", slot, val, dict(self.vc[q]), o)
        for r in writes:
            r.w = ev
            r.r = {}
        for r in reads:
            r.r[("d", slot)] = ev
        return o

    def finish(self, eng, all_res):
        o = Op(eng, len(self.ops[eng]), "nop", "finish")
        self.ops[eng].append(o)
        for r in all_res:
            self._need(o, r.w)
            for ev in r.r.values():
                self._need(o, ev)
        return o

    def emit(self):
        nc = self.nc
        sigcount = {}
        for e in ENGS:
            c = 0
            lst = []
            for o in self.ops[e]:
                if o.signal:
                    c += 1
                lst.append(c)
            sigcount[e] = lst
        engobj = {"pe": "tensor", "act": "scalar", "dve": "vector", "pool": "gpsimd", "sp": "sync"}

        def run(e, eng):
            for o in self.ops[e]:
                for ev in o.waits:
                    if ev.kind == "c":
                        eng.wait_ge(self.sem[ev.eng], sigcount[ev.eng][ev.idx])
                    else:
                        eng.wait_ge(self.dsem[ev.eng], ev.idx)
                if o.dma is not None:
                    slot, out, in_, kw = o.dma
                    eng.dma_start(out=out, in_=in_, **kw).then_inc(self.dsem[slot], 16)
                elif o.fn == "nop":
                    pass
                else:
                    ins = o.fn(eng)
                    if o.signal:
                        ins.then_inc(self.sem[e], 1)

        with nc.Block() as block:
            @block.tensor
            def _(eng):
                run("pe", eng)

            @block.scalar
            def _(eng):
                run("act", eng)

            @block.vector
            def _(eng):
                run("dve", eng)

            @block.gpsimd
            def _(eng):
                run("pool", eng)

            @block.sync
            def _(eng):
                run("sp", eng)


def build(T=T, taps=None):
    NT = T // TT
    taps = taps or ()
    nc = bass.Bass("TRN2", target_bir_lowering=False)
    dr = lambda name, shape, dt=F32, kind="ExternalInput": nc.dram_tensor(name, shape, dt, kind=kind).ap()
    x = dr("x", [T, D])
    p_in = dr("p", [T, PLE])
    attn_pre = dr("attn_pre_norm", [D])
    w_in = dr("w_in", [D, INC])
    hg_gamma = dr("hg_lower_gamma", [2, HGW])
    hg_norm = dr("hg_out_norm", [HGW])
    sb_norm = dr("sb_out_norm", [SBW])
    w_out = dr("w_out", [D, D])
    attn_post = dr("attn_post_norm", [D])
    ffn_pre = dr("ffn_pre_norm", [D])
    w_gu = dr("w_gate_up", [D, 2 * DFF])
    w_down = dr("w_down", [DFF, D])
    ffn_post = dr("ffn_post_norm", [D])
    ple_proj = dr("ple_proj", [PLE, D])
    ple_gate = dr("ple_gate", [D, D])
    y = dr("y", [T, D], F32, "ExternalOutput")
    w_in_b = dr("w_in_b", [D, INC], BF16, "Internal")
    w_out_b = dr("w_out_b", [D, D], BF16, "Internal")
    w_gu_b = dr("w_gu_b", [D, 2 * DFF], BF16, "Internal")
    w_down_b = dr("w_down_b", [DFF, D], BF16, "Internal")
    ple_proj_b = dr("ple_proj_b", [PLE, D], BF16, "Internal")
    ple_gate_b = dr("ple_gate_b", [D, D], BF16, "Internal")

    with contextlib.ExitStack() as st:
        pg = Prog(nc, st)
        sb = lambda name, shape, dt: st.enter_context(nc.sbuf_tensor(name, shape, dt))
        ps = lambda name, shape, dt: st.enter_context(nc.psum_tensor(name, shape, dt))
        R = pg.res

        def ACT(func, out, in_, reads, writes, **kw):
            pg.op("act", lambda e: e.activation(out=out, in_=in_, func=func, **kw), reads=reads, writes=writes)

        def MM(out, lhsT, rhs, start, stop, reads, writes, skip=False):
            pg.op("pe", lambda e: e.matmul(out, lhsT=lhsT, rhs=rhs, start=start, stop=stop, skip_group_check=skip),
                  reads=reads, writes=writes)

        def TT_(eng, out, in0, in1, op, reads, writes):
            pg.op(eng, lambda e: e.tensor_tensor(out=out, in0=in0, in1=in1, op=op), reads=reads, writes=writes)

        def TS(eng, out, in0, s1, s2, op0, op1, reads, writes):
            if s2 is None:
                pg.op(eng, lambda e: e.tensor_scalar(out=out, in0=in0, scalar1=s1, scalar2=None, op0=op0), reads=reads, writes=writes)
            else:
                pg.op(eng, lambda e: e.tensor_scalar(out=out, in0=in0, scalar1=s1, scalar2=s2, op0=op0, op1=op1), reads=reads, writes=writes)

        def STT(eng, out, in0, scalar, in1, op0, op1, reads, writes):
            pg.op(eng, lambda e: e.scalar_tensor_tensor(out=out, in0=in0, scalar=scalar, in1=in1, op0=op0, op1=op1),
                  reads=reads, writes=writes)

        def CP(eng, out, in_, reads, writes):
            if eng == "act":
                pg.op("act", lambda e: e.copy(out=out, in_=in_), reads=reads, writes=writes)
            else:
                pg.op(eng, lambda e: e.tensor_copy(out=out, in_=in_), reads=reads, writes=writes)

        def MEMSET(eng, ap, val, writes, reads=()):
            pg.op(eng, lambda e: e.memset(ap, val), reads=reads, writes=writes)

        tap_outs = []

        def TAP(name, ap, reads, ti=0, only_ti=0):
            if name not in taps or ti != only_ti:
                return
            shape = list(ap.shape)
            d = nc.dram_tensor("tap_" + name, shape, ap.dtype, kind="ExternalOutput").ap()
            rr = R()
            pg.dma("sp", d, ap, reads=reads, writes=[rr])
            tap_outs.append(rr)

        psall = ps("psall", [P, 8 * 512], F32)
        bk = [psall[:, i * 512:(i + 1) * 512] for i in range(8)]
        rb = [R("bk%d" % i, excl=True) for i in range(8)]
        bkT = psall[:, 0:512].bitcast(BF16)
        r_bkT = rb[0]

        ident = sb("ident", [P, P], BF16); r_ident = R()
        ones_bf = sb("ones_bf", [P, P], BF16); r_ones = R()
        zeros_bf = sb("zeros_bf", [P, P], BF16); r_zeros = R()
        tmpf = sb("tmpf", [P, P], F32); r_tmpf = R()
        tri_bf = sb("tri_bf", [P, P], BF16); r_tri = R()
        su_bf = sb("su_bf", [P, P], BF16); r_su = R()
        cmask = sb("cmask", [P, P], BF16); r_cmask = R()
        mh = sb("mh", [P, 4, 64], F32); r_mh = R()
        m1 = sb("m1", [P, 64], F32); m2 = sb("m2", [P, 64], F32); m3 = sb("m3", [P, 64], F32); r_m = R()
        rst = sb("rst", [P, 512], F32); r_rst = R()
        wbc_post = sb("wbc_post", [P, D], F32); r_wbc_post = R()
        wbc_fpost = sb("wbc_fpost", [P, D], F32); r_wbc_fpost = R()
        wbc_pre = sb("wbc_pre", [P, D], F32); r_wbc_pre = R()
        wbc_fpre = sb("wbc_fpre", [P, D], F32); r_wbc_fpre = R()
        rows = sb("rows", [32, P], F32); r_rows = R()
        pv = sb("pv", [P, 32], F32); r_pv = R()
        lbt = sb("lbt", [P, 8], F32); r_lb = R()
        pproj = sb("pproj", [P, 2, D], BF16); r_pproj = R()

        cast_res = {}

        def cast(tag, src, dst, r0, r1, c0, c1):
            rr = R("wb_" + tag)
            w = c1 - c0
            bw = 256 if w % 256 == 0 else 128
            pg.dma("pool", dst[r0:r1, c0:c1].rearrange("r (a b) -> r a b", b=bw), src[r0:r1, c0:c1].rearrange("r (a b) -> r a b", b=bw),
                   writes=[rr])
            cast_res[tag] = [rr]

        IN_ORDER = [("hi", 1024), ("hg", 1536), ("sq", 2048), ("sk", 2560), ("sv", 3072), ("hq", 0), ("hf", 512)]
        GROUPS = [(0, 4), (4, 4), (8, 4), (12, 4), (16, 4), (20, 2)]
        DGRP = [(0, 8), (8, 8), (16, 6)]
        for nm, c0 in IN_ORDER:
            cast("in_" + nm, w_in, w_in_b, 0, D, c0, c0 + 512)
        MEMSET("pool", ident[:], 0.0, [r_ident])
        pg.op("pool", lambda e: e.affine_select(out=ident[:], in_=ident[:], compare_op=ALU.not_equal, fill=1.0, base=0,
                                                pattern=[[-1, P]], channel_multiplier=1), reads=[r_ident], writes=[r_ident])
        MEMSET("pool", ones_bf[:], 1.0, [r_ones])
        MEMSET("pool", zeros_bf[:], 0.0, [r_zeros])
        MEMSET("pool", tmpf[:], 1.0, [r_tmpf])
        pg.op("pool", lambda e: e.affine_select(out=tmpf[:], in_=tmpf[:], compare_op=ALU.is_ge, fill=0.0, base=0,
                                                pattern=[[-1, P]], channel_multiplier=1), reads=[r_tmpf], writes=[r_tmpf])
        CP("pool", tri_bf[:], tmpf[:], [r_tmpf], [r_tri])
        MEMSET("pool", tmpf[:], 1.0, [r_tmpf], reads=[r_tmpf])
        pg.op("pool", lambda e: e.affine_select(out=tmpf[:], in_=tmpf[:], compare_op=ALU.is_gt, fill=0.0, base=0,
                                                pattern=[[1, P]], channel_multiplier=-1), reads=[r_tmpf], writes=[r_tmpf])
        CP("pool", su_bf[:], tmpf[:], [r_tmpf], [r_su])
        CP("pool", cmask[:], tmpf[:], [r_tmpf], [r_cmask])
        MEMSET("pool", m1[:], 1.0, [r_m])
        MEMSET("pool", m2[:], 1.0, [r_m])
        MEMSET("pool", m3[:], 1.0, [r_m])
        pg.op("pool", lambda e: e.affine_select(out=m1[:], in_=m1[:], compare_op=ALU.is_ge, fill=0.0, base=0,
                                                pattern=[[1, 64]], channel_multiplier=-1), reads=[r_m], writes=[r_m])
        pg.op("pool", lambda e: e.affine_select(out=m2[:], in_=m2[:], compare_op=ALU.is_ge, fill=0.0, base=64,
                                                pattern=[[1, 64]], channel_multiplier=-1), reads=[r_m], writes=[r_m])
        pg.op("pool", lambda e: e.affine_select(out=m3[:], in_=m3[:], compare_op=ALU.is_ge, fill=0.0, base=-64,
                                                pattern=[[0, 64]], channel_multiplier=1), reads=[r_m], writes=[r_m])
        TT_("pool", m2[:], m2[:], m3[:], ALU.mult, [r_m], [r_m])
        TT_("pool", m1[:], m1[:], m2[:], ALU.add, [r_m], [r_m])
        for m in range(4):
            CP("pool", mh[:, m, :], m1[:], [r_m], [r_mh])
        MEMSET("pool", rst[:], 1.0, [r_rst])
        MEMSET("pool", rst[:].rearrange("p (c t) -> p c t", t=64)[:, :, 0:1], 0.0, [r_rst], reads=[r_rst])

        pg.dma("sp", wbc_post[:], attn_post.partition_broadcast(P), writes=[r_wbc_post])
        pg.dma("sp", wbc_fpost[:], ffn_post.partition_broadcast(P), writes=[r_wbc_fpost])
        pg.dma("sp", wbc_pre[:], attn_pre.partition_broadcast(P), writes=[r_wbc_pre])
        pg.dma("sp", wbc_fpre[:], ffn_pre.partition_broadcast(P), writes=[r_wbc_fpre])
        pg.dma("sp", rows[0:8, :], hg_gamma.rearrange("r (c p) -> (r c) p", p=P), writes=[r_rows])
        r_rows2 = R(); r_rows3 = R(); r_rows4 = R(); r_rows5 = R()
        pg.dma("sp", rows[8:12, :], hg_norm.rearrange("(c p) -> c p", p=P), writes=[r_rows2])
        pg.dma("sp", rows[12:16, :], sb_norm.rearrange("(c p) -> c p", p=P), writes=[r_rows3])
        pg.dma("sp", rows[16:24, :], attn_pre.rearrange("(c p) -> c p", p=P), writes=[r_rows4])
        pg.dma("sp", rows[24:32, :], ffn_pre.rearrange("(c p) -> c p", p=P), writes=[r_rows5])
        identf32 = sb("identf32", [32, 32], F32); r_identf32 = R()
        MEMSET("pool", identf32[:], 0.0, [r_identf32])
        pg.op("pool", lambda e: e.affine_select(out=identf32[:], in_=identf32[:], compare_op=ALU.not_equal, fill=1.0, base=0,
                                                pattern=[[-1, 32]], channel_multiplier=1), reads=[r_identf32], writes=[r_identf32])
        MM(bk[1][:, 0:32], rows[:, :], identf32[:, :], True, True,
           [r_rows, r_rows2, r_rows3, r_rows4, r_rows5, r_identf32], [rb[1]])
        CP("dve", pv[:], bk[1][:, 0:32], [rb[1]], [r_pv])
        TT_("dve", lbt[:, 0:4], pv[:, 0:4], pv[:, 4:8], ALU.subtract, [r_pv], [r_lb])
        ACT(AF.Sigmoid, lbt[:, 0:4], lbt[:, 0:4], [r_lb], [r_lb])
        TS("dve", lbt[:, 4:8], lbt[:, 0:4], -1.0, 1.0, ALU.mult, ALU.add, [r_lb], [r_lb])

        cast("pproj", ple_proj, ple_proj_b, 0, PLE, 0, D)
        for half in range(2):
            cast("out%d" % half, w_out, w_out_b, 0, D, half * 512, (half + 1) * 512)
        for g, (f0, nf) in enumerate(GROUPS):
            cast("g%d" % g, w_gu, w_gu_b, 0, D, f0 * P, (f0 + nf) * P)
            cast("u%d" % g, w_gu, w_gu_b, 0, D, DFF + f0 * P, DFF + (f0 + nf) * P)
        for half in range(2):
            for fg, (f0, nf) in enumerate(DGRP):
                cast("d%d_%d" % (half, fg), w_down, w_down_b, f0 * P, (f0 + nf) * P, half * 512, (half + 1) * 512)
        for half in range(2):
            cast("pg%d" % half, ple_gate, ple_gate_b, 0, D, half * 512, (half + 1) * 512)

        pg.dma("sp", pproj[:], ple_proj_b.rearrange("(kc p) n -> p kc n", p=P), reads=cast_res["pproj"], writes=[r_pproj])

        NB = T // P
        skT = sb("skT", [P, 4, T], BF16); r_skT = [R() for _ in range(NT)]
        svb = sb("svb", [P, NB, SBW], BF16); r_sv = [R() for _ in range(NT)]
        xt = sb("xt", [P, 4, D], F32); r_xt = [R() for _ in range(4)]
        xn = [sb("xn%d" % i, [P, D], BF16) for i in range(2)]; r_xn = [R(), R()]
        junk = xn[1]; r_junk = r_xn[1]
        ssq = sb("ssq", [P, 8], F32); r_ssq = R()
        rstd = sb("rstd", [P, 8], F32); r_rstd = R()
        uT = sb("uT", [P, 8, TT], BF16); r_uT = [R() for _ in range(8)]
        catH = sb("catH", [P, 4, TT], BF16); r_catH = [R() for _ in range(4)]
        NSL = 3
        hsl = [sb("hsl%d" % i, [P, 8, 512], BF16) for i in range(NSL)]; r_h = [R() for _ in range(NSL)]
        Vt = sb("Vt", [P, 4, HGW], BF16); r_V = R()
        hgT = sb("hgT", [P, 4, TT], BF16); r_hgT = R()
        sqT = sb("sqT", [P, 4, TT], BF16); r_sqT = R()
        carry = sb("carry", [P, 4, P], F32); r_carry = R()
        sqs = sb("sqs", [P, TT], BF16); r_sqs = R()
        sqh = sqs; r_sqh = r_sqs
        catS = sb("catS", [P, 4, TT], BF16); r_catS = [R() for _ in range(4)]
        lnr = sb("lnr", [P, TT], F32); r_lnr = R()
        pt = sb("pt", [P, PLE], F32); r_pt = R()
        pn = sb("pn", [P, PLE], BF16); r_pn = R()
        pT = sb("pT", [P, 2, TT], BF16); r_pT = R()
        small = sb("small", [P, 32], F32); r_small = R()
        small_b = sb("small_b", [P, 24], F32); r_small_b = R()
        smalls = [(small, r_small), (small_b, r_small_b)]
        attn_scr = sb("attn_scr", [P, 6144], BF16)
        e_all = attn_scr[:, 0:2048].rearrange("p (a c t) -> p a c t", a=2, c=2)
        L_all = attn_scr[:, 2048:4096].rearrange("p (a c t) -> p a c t", a=2, c=2)
        X_one = attn_scr[:, 4096:5120].rearrange("p (c t) -> p c t", c=2)
        A_one = attn_scr[:, 5120:6144].rearrange("p (c t) -> p c t", c=2)
        e3 = sb("e3", [P, 2, 512], BF16)
        e_bufs = [e_all[:, 0], e_all[:, 1], e3[:]]
        r_e = [R(), R(), R()]; r_L = [R(), R()]; r_X = R(); r_A = R()
        ybuf = attn_scr[:, 0:4096].bitcast(F32).rearrange("p (j t) -> p j t", j=4); r_ybuf = R()
        tmp = attn_scr[:, 4096:5120].bitcast(F32); r_tmp = R()
        sg = attn_scr[:, 5120:6144].bitcast(F32); r_sg = R()
        actT = sb("actT", [P, 22, TT], BF16); r_actT = R()
        scr_f = actT[:].rearrange("p a b -> p (a b)").bitcast(F32)
        scr_b = actT[:].rearrange("p a b -> p (a b)")
        t1 = scr_f[:, 0:512]; t2 = scr_f[:, 512:1024]; t3 = scr_f[:, 1024:1536]
        qf = scr_f[:, 1536:2048]; ff = scr_f[:, 2048:2560]
        bo = 2560 * 2
        Qt_bf = scr_b[:, bo:bo + 512]; Qp_bf = scr_b[:, bo + 512:bo + 1024]
        Kt_bf = scr_b[:, bo + 1024:bo + 1536]; Kh_bf = scr_b[:, bo + 1536:bo + 2048]
        KhT = scr_b[:, bo + 2048:bo + 2560].rearrange("p (a b) -> p a b", b=P)
        scT = scr_b[:, bo + 2560:bo + 2816]
        Sbf = scr_b[:, bo + 2816:bo + 2816 + 1024].rearrange("p (a b) -> p a b", b=P)
        Sall = scr_f[:, 4480:5632].rearrange("p (a b) -> p a b", b=P); r_Sall = R()
        r_t1 = R(); r_t2 = R(); r_t3 = R(); r_qf = R(); r_ff = R()
        r_Qt = R(); r_Qp = R(); r_Kt = R(); r_Kh = R(); r_KhT = R(); r_scT = R(); r_Sbf = R()
        r_X2 = R()
        pg.alias(r_X2, r_pT)
        X_bufs = [X_one, pT[:]]
        r_Xs = [r_X, r_X2]
        hg_scr = [r_t1, r_t2, r_t3, r_qf, r_ff, r_Qt, r_Qp, r_Kt, r_Kh, r_KhT, r_scT, r_Sbf, r_Sall]
        fence_t = sb("fence_t", [P, 2], F32)
        fence_set = hg_scr + [r_actT, r_ybuf, r_tmp, r_sg, r_X, r_A] + r_e + r_L

        def fence():
            MEMSET("pool", fence_t[:, 0:1], 0.0, fence_set)

        MEMSET("pool", carry[:], 0.0, [r_carry])

        wv_in = w_in_b.rearrange("(kc p) n -> p kc n", p=P)
        wv_out = w_out_b.rearrange("(kc p) n -> p kc n", p=P)
        wv_gu = w_gu_b.rearrange("(kc p) n -> p kc n", p=P)
        wv_dn = w_down_b.rearrange("(kc p) n -> p kc n", p=P)
        wv_pg = ple_gate_b.rearrange("(kc p) n -> p kc n", p=P)
        def tile_blocks():
            bl = []
            for nm, c0 in IN_ORDER:
                bl.append(("in_" + nm, wv_in[:, :, c0:c0 + 512], 8, 512, cast_res["in_" + nm]))
            for half in range(2):
                bl.append(("out%d" % half, wv_out[:, :, half * 512:(half + 1) * 512], 8, 512, cast_res["out%d" % half]))
            for g, (f0, nf) in enumerate(GROUPS):
                bl.append(("g%d" % g, wv_gu[:, :, f0 * P:(f0 + nf) * P], 8, nf * P, cast_res["g%d" % g]))
                bl.append(("u%d" % g, wv_gu[:, :, DFF + f0 * P:DFF + (f0 + nf) * P], 8, nf * P, cast_res["u%d" % g]))
            for half in range(2):
                for fg, (f0, nf) in enumerate(DGRP):
                    bl.append(("d%d_%d" % (half, fg), wv_dn[:, f0:f0 + nf, half * 512:(half + 1) * 512], nf, 512, cast_res["d%d_%d" % (half, fg)]))
            for half in range(2):
                bl.append(("pg%d" % half, wv_pg[:, :, half * 512:(half + 1) * 512], 8, 512, cast_res["pg%d" % half]))
            return bl

        blocks = []
        for ti in range(NT):
            blocks.extend(tile_blocks())
        wstate = {"issued": 0, "next": 0, "released": 0}

        def _issue_upto(n):
            while wstate["issued"] < min(len(blocks), n):
                j = wstate["issued"]
                assert j - NSL < wstate["released"], "slot still live"
                _, view, nk, ncol, rsrc = blocks[j]
                pg.dma("sp", hsl[j % NSL][:, 0:nk, 0:ncol], view, reads=rsrc, writes=[r_h[j % NSL]])
                wstate["issued"] += 1

        def wnext(tag):
            i = wstate["next"]
            wstate["next"] += 1
            assert blocks[i][0] == tag, (blocks[i][0], tag)
            _issue_upto(i + 1)
            return hsl[i % NSL], r_h[i % NSL]

        def wrel(k=1):
            wstate["released"] += k
            assert wstate["released"] <= wstate["next"]
            _issue_upto(wstate["released"] + NSL)

        bank_rot = {"i": 0}

        def nextbank(choices):
            b = choices[bank_rot["i"] % len(choices)]
            bank_rot["i"] += 1
            return b

        def fm_mm(b, W, rW, col0, ncols=P):
            for k in range(8):
                MM(bk[b][0:ncols, :], W[:, k, col0:col0 + ncols], uT[:, k, :], k == 0, k == 7, [rW, r_uT[k]], [rb[b]])

        def tm_mm(b, j, W, rW, col0, ncols=512):
            for k in range(8):
                MM(bk[b][:, 0:ncols], uT[:, k, j * P:(j + 1) * P], W[:, k, col0:col0 + ncols], k == 0, k == 7,
                   [rW, r_uT[k]], [rb[b]])

        bkT2 = psall[:, 7 * 512:8 * 512].bitcast(BF16)
        tr_banks = [(bkT, rb[0]), (bkT2, rb[7])]

        def norm_transpose(wbc, r_wbc, src_is_norm=True):
            if src_is_norm:
                for j in range(4):
                    ACT(AF.Square, junk[:], xt[:, j, :], [r_xt[j]], [r_junk, r_ssq], accum_out=ssq[:, j:j + 1])
                    ACT(AF.Ln, rstd[:, j:j + 1], ssq[:, j:j + 1], [r_ssq], [r_rstd], scale=1.0 / D, bias=EPS)
                    ACT(AF.Exp, rstd[:, j:j + 1], rstd[:, j:j + 1], [r_rstd], [r_rstd], scale=-0.5)
            for j in range(4):
                xb = j % 2
                tb, rtb = tr_banks[j % 2]
                if src_is_norm:
                    STT("dve", xn[xb][:], xt[:, j, :], rstd[:, j:j + 1], wbc[:], ALU.mult, ALU.mult, [r_xt[j], r_rstd, r_wbc], [r_xn[xb]])
                else:
                    CP("act" if j % 2 == 0 else "dve", xn[xb][:], xt[:, j, :], [r_xt[j]], [r_xn[xb]])
                for k in range(8):
                    pg.op("pe", lambda e, k=k, xb=xb, tb=tb: e.transpose(out=tb[:, k * P:(k + 1) * P], in_=xn[xb][:, k * P:(k + 1) * P],
                                                                          identity=ident[:]), reads=[r_xn[xb], r_ident], writes=[rtb])
                CP("act" if j % 2 == 0 else "dve", uT[:, :, j * P:(j + 1) * P], tb.rearrange("p (k t) -> p k t", k=8), [rtb], r_uT)

        acc_hg = xn[0][:].bitcast(F32)
        acc_sb = xn[1][:].bitcast(F32)

        def bc_rstd(acc, r_acc, width, buf, rbuf):
            ACT(AF.Ln, lnr[:], acc, [r_acc], [r_lnr], scale=1.0 / width, bias=EPS)
            ACT(AF.Exp, lnr[:], lnr[:], [r_lnr], [r_lnr], scale=-0.5)
            for c in range(4):
                TT_("dve", buf[:, c, :], buf[:, c, :], lnr[:], ALU.mult, [rbuf[c], r_lnr], [rbuf[c]])

        for ti in range(NT):
            tok0 = ti * TT
            if ti == 0:
                for j in range(4):
                    pg.dma("sp", xt[:, j, :], x[tok0 + j * P:tok0 + (j + 1) * P, :], writes=[r_xt[j]])
            norm_transpose(wbc_pre, r_wbc_pre)
            TAP('uT1', uT[:], r_uT, ti)
            fence()
            W, rW = wnext("in_hi")
            for j in range(4):
                b = nextbank([1, 2])
                tm_mm(b, j, W, rW, 0)
                CP("act", Vt[:, j, :], bk[b][:], [rb[b]], [r_V])
            wrel()
            W, rW = wnext("in_hg")
            for c in range(4):
                b = nextbank([1, 2])
                fm_mm(b, W, rW, c * P)
                ACT(AF.Silu, hgT[:, c, :], bk[b][:], [rb[b]], [r_hgT])
            wrel()
            W, rW = wnext("in_sq")
            for c in range(4):
                b = nextbank([1, 2])
                fm_mm(b, W, rW, c * P)
                CP("dve", sqT[:, c, :], bk[b][:], [rb[b]], [r_sqT])
            wrel()
            W, rW = wnext("in_sk")
            for c in range(4):
                b = nextbank([1, 2])
                fm_mm(b, W, rW, c * P)
                CP("dve", skT[:, c, tok0:tok0 + TT], bk[b][:], [rb[b]], [r_skT[ti]])
            wrel()
            W, rW = wnext("in_sv")
            for j in range(4):
                b = nextbank([1, 2])
                tm_mm(b, j, W, rW, 0)
                CP("act", svb[:, 4 * ti + j, :], bk[b][:], [rb[b]], [r_sv[ti]])
            wrel()
            TAP('Vt', Vt[:], [r_V], ti); TAP('hgT', hgT[:], [r_hgT], ti); TAP('sqT', sqT[:], [r_sqT], ti)
            TAP('skT', skT[:, :, 0:512], [r_skT[0]], ti); TAP('sv', svb[:, 0:4, :], [r_sv[0]], ti)
            Wq, rWq = wnext("in_hq")
            Wf, rWf = wnext("in_hf")
            b3 = lambda ap: ap.rearrange("p (c t) -> p c t", t=64)
            HQ, HF, HX = 2, 3, 7
            bkX = psall[:, HX * 512:(HX + 1) * 512].bitcast(BF16)

            HINT = [2.5, 2.5, 3.0, 3.0, 1.5, 2.0, 2.0, 2.0, 2.0, 1.5, 1.0, 2.0, 2.0, 1.0, 1.5, 2.0, 1.5, 2.5, 1.5, 1.0, 1.0]

            def hg_gen():
                for h in range(4):
                    sm, r_sm = smalls[h % 2]
                    for k in range(8):
                        MM(bk[HQ][:, :], Wq[:, k, h * P:(h + 1) * P], uT[:, k, :], k == 0, k == 7, [rWq, r_uT[k]], [rb[HQ]])
                        if k == 3:
                            yield 1.3
                    yield HINT[0]
                    for k in range(8):
                        MM(bk[HF][:, :], Wf[:, k, h * P:(h + 1) * P], uT[:, k, :], k == 0, k == 7, [rWf, r_uT[k]], [rb[HF]])
                        if k == 3:
                            ACT(AF.Exp, t3, bk[HQ][:], [rb[HQ]], [r_t3], scale=-1.0)
                            yield 1.3
                    yield HINT[1]
                    ACT(AF.Exp, t1, bk[HF][:], [rb[HF]], [r_t1], scale=-1.0)
                    TS("dve", t3, t3, 1.0, None, ALU.add, None, [r_t3], [r_t3])
                    pg.op("dve", lambda e: e.reciprocal(out=t3, in_=t3), reads=[r_t3], writes=[r_t3])
                    TT_("dve", qf, bk[HQ][:], t3, ALU.mult, [rb[HQ], r_t3], [r_qf])
                    yield HINT[2]
                    TS("dve", t1, t1, 1.0, None, ALU.add, None, [r_t1], [r_t1])
                    pg.op("dve", lambda e: e.reciprocal(out=t1, in_=t1), reads=[r_t1], writes=[r_t1])
                    TS("dve", ff, t1, lbt[:, 4 + h:5 + h], lbt[:, h:h + 1], ALU.mult, ALU.add, [r_t1, r_lb], [r_ff])
                    yield HINT[3]
                    yield 3.0
                    ACT(AF.Ln, t1, ff, [r_ff], [r_t1])
                    TS("pool", ff, ff, -1.0, 1.0, ALU.mult, ALU.add, [r_ff], [r_ff])
                    yield HINT[4]
                    pg.op("dve", lambda e: e.tensor_tensor_scan(out=t2, data0=rst[:], data1=t1, initial=0.0, op0=ALU.mult, op1=ALU.add),
                          reads=[r_rst, r_t1], writes=[r_t2])
                    TT_("dve", b3(t1), b3(t2), b3(t2)[:, :, 31:32].broadcast_to([P, 8, 64]), ALU.subtract, [r_t2], [r_t1])
                    yield HINT[5]
                    yield 2.5
                    ACT(AF.Exp, sm[:, 0:8], b3(t2)[:, :, 31], [r_t2], [r_sm])
                    ACT(AF.Exp, t3, t1, [r_t1], [r_t3])
                    yield HINT[6]
                    ACT(AF.Exp, t1, t1, [r_t1], [r_t1], scale=-1.0)
                    CP("pool", sm[:, 8:16], b3(t3)[:, :, 63], [r_t3], [r_sm])
                    TT_("pool", sm[:, 16:24], sm[:, 8:16], sm[:, 0:8], ALU.mult, [r_sm], [r_sm])
                    TT_("dve", qf, qf, t3, ALU.mult, [r_qf, r_t3], [r_qf])
                    yield HINT[7]
                    TT_("pool", ff, ff, t1, ALU.mult, [r_ff, r_t1], [r_ff])
                    CP("pool", Qt_bf, qf, [r_qf], [r_Qt])
                    TT_("dve", b3(Qp_bf), b3(qf), sm[:, 0:8].unsqueeze(2).broadcast_to([P, 8, 64]), ALU.mult, [r_qf, r_sm], [r_Qp])
                    yield HINT[8]
                    CP("pool", Kt_bf, ff, [r_ff], [r_Kt])
                    TT_("dve", b3(Kh_bf), b3(ff), sm[:, 8:16].unsqueeze(2).broadcast_to([P, 8, 64]), ALU.mult, [r_ff, r_sm], [r_Kh])
                    yield HINT[9]
                    yield HINT[10]
                    for blk in range(4):
                        pg.op("pe", lambda e, blk=blk: e.transpose(out=bkX[:, blk * P:(blk + 1) * P], in_=Kh_bf[:, blk * P:(blk + 1) * P],
                                                                    identity=ident[:]), reads=[r_Kh, r_ident], writes=[rb[HX]])
                    yield 1.0
                    for c in range(8):
                        m, pr = c // 2, 64 * (c % 2)
                        MM(bk[HQ][pr:pr + 64, m * 64:(m + 1) * 64], Kt_bf[:, c * 64:(c + 1) * 64], Qt_bf[:, c * 64:(c + 1) * 64],
                           True, True, [r_Kt, r_Qt], [rb[HQ]])
                    yield HINT[11]
                    CP("dve", KhT, bkX[:, 0:512].rearrange("p (a b) -> p a b", b=P), [rb[HX]], [r_KhT])
                    TT_("dve", scT, bk[HQ][:, 0:256], mh[:].rearrange("p a b -> p (a b)"), ALU.mult, [rb[HQ], r_mh], [r_scT])
                    CP("pool", Sall[:, 0, :], carry[:, h, :], [r_carry], [r_Sall])
                    yield HINT[12]
                    yield HINT[13]
                    for c in range(8):
                        m, pr = c // 2, 64 * (c % 2)
                        ub = HF if c % 2 == 0 else HX
                        MM(bk[ub][:, m * P:(m + 1) * P], KhT[pr:pr + 64, m, :], Vt[pr:pr + 64, m, h * P:(h + 1) * P],
                           True, True, [r_KhT, r_V], [rb[ub]])
                    yield HINT[14]
                    for c in range(8):
                        m = c // 2
                        ub = HF if c % 2 == 0 else HX
                        STT("dve", Sall[:, c + 1, :], Sall[:, c, :], sm[:, 16 + c:17 + c], bk[ub][:, m * P:(m + 1) * P],
                            ALU.mult, ALU.add, [r_Sall, r_sm, rb[ub]], [r_Sall])
                        if c % 2 == 1:
                            yield 2.5
                    CP("pool", carry[:, h, :], Sall[:, 8, :], [r_Sall], [r_carry])
                    CP("pool", Sbf, Sall[:, 0:8, :], [r_Sall], [r_Sbf])
                    yield HINT[15]
                    yield HINT[16]
                    for c in range(8):
                        m, pr = c // 2, 64 * (c % 2)
                        MM(bk[HQ][:, c * 64:(c + 1) * 64], Sbf[:, c, :], Qp_bf[:, c * 64:(c + 1) * 64], True, False,
                           [r_Sbf, r_Qp], [rb[HQ]])
                        MM(bk[HQ][:, c * 64:(c + 1) * 64], Vt[pr:pr + 64, m, h * P:(h + 1) * P], scT[pr:pr + 64, m * 64:(m + 1) * 64],
                           False, True, [r_V, r_scT], [rb[HQ]])
                        if c == 3:
                            yield 1.0
                    yield HINT[17]
                    STT("dve", catH[:, h, :], bk[HQ][:], pv[:, 8 + h:9 + h], hgT[:, h, :], ALU.mult, ALU.mult,
                        [rb[HQ], r_pv, r_hgT], [r_catH[h]])
                    ACT(AF.Square, sqh[:], bk[HQ][:], [rb[HQ]], [r_sqh])
                    yield HINT[18]
                    MM(bk[HF][:], ones_bf[:], sqh[:], True, True, [r_ones, r_sqh], [rb[HF]])
                    yield HINT[19]
                    if h == 0:
                        CP("dve", acc_hg, bk[HF][:], [rb[HF]], [r_xn[0]])
                    else:
                        TT_("dve", acc_hg, acc_hg, bk[HF][:], ALU.add, [rb[HF], r_xn[0]], [r_xn[0]])
                    yield HINT[20]

            hgen = hg_gen()
            hg_state = {"done": False, "budget": 0.0, "need": 0.0}

            def hg_advance(dt):
                hg_state["budget"] += dt
                while not hg_state["done"] and hg_state["budget"] >= hg_state["need"]:
                    hg_state["budget"] -= hg_state["need"]
                    try:
                        hg_state["need"] = next(hgen)
                    except StopIteration:
                        hg_state["done"] = True

            kbs = list(range(4 * ti + 3, -1, -1))
            nst = len(kbs)
            SLOT_US = 3.3
            HG_TOTAL = 4 * (sum(HINT) + 4 * 2.5 + 1.3 * 2 + 2.0 + 5.5)
            per_slot = SLOT_US * max(1.0, HG_TOTAL / (4 * nst * SLOT_US))
            psC2 = psall[:, 4 * 512:6 * 512].rearrange("p (c t) -> p c t", c=2)
            psZ2 = psall[:, 0:1024].rearrange("p (c t) -> p c t", c=2)

            def t0_of(k):
                return max(0, kbs[k] - 4 * ti) * P

            NG = 4 * nst

            def Zs(g):
                pp, k = divmod(g, nst)
                kb, t0 = kbs[k], t0_of(k)
                for hh in range(2):
                    pb = 64 * hh
                    MM(bk[hh][:, t0:512], skT[pb:pb + 64, pp, kb * P:(kb + 1) * P], sqT[pb:pb + 64, pp, t0:512], True, True,
                       [r_skT[kb // 4], r_sqT], [rb[hh]])

            def Es(g):
                pp, k = divmod(g, nst)
                kb, pe3, t0 = kbs[k], g % 3, t0_of(k)
                eb = e_bufs[pe3]
                ACT(AF.Exp, eb[:, :, t0:512], psZ2[:, :, t0:512], [rb[0], rb[1]], [r_e[pe3]], scale=0.125)
                if kb >= 4 * ti:
                    TT_("pool", eb[:, :, t0:t0 + P], eb[:, :, t0:t0 + P],
                        cmask[:].unsqueeze(1).broadcast_to([P, 2, P]), ALU.mult, [r_e[pe3], r_cmask], [r_e[pe3]])

            def Ls(g):
                k = g % nst
                par, t0 = g % 2, t0_of(k)
                ACT(AF.Ln, L_all[:, par, :, t0:512], e_bufs[g % 3][:, :, t0:512], [r_e[g % 3]], [r_L[par]], bias=1.0)

            def Ts(g):
                k = g % nst
                par, t0 = g % 2, t0_of(k)
                if k == 0:
                    for _z in range(4):
                        MM(bk[4][:, _z * P:(_z + 1) * P], zeros_bf[:], zeros_bf[:], _z == 0, False, [r_zeros], [rb[4]], skip=True)
                    for _z in range(4):
                        MM(bk[5][:, _z * P:(_z + 1) * P], zeros_bf[:], zeros_bf[:], _z == 0, False, [r_zeros], [rb[5]], skip=True)
                for hh in range(2):
                    MM(bk[4 + hh][:, t0:512], tri_bf[:], L_all[:, par, hh, t0:512], False, False, [r_tri, r_L[par]], [rb[4 + hh]], skip=True)

            def Xs(g):
                k = g % nst
                t0 = t0_of(k)
                ACT(AF.Exp, X_bufs[g % 2][:, :, t0:512], psC2[:, :, t0:512], [rb[4], rb[5]], [r_Xs[g % 2]], scale=-1.0)

            def Ss(g):
                k = g % nst
                par, t0 = g % 2, t0_of(k)
                if k == nst - 1:
                    return
                for hh in range(2):
                    MM(bk[4 + hh][:, t0:512], su_bf[:], L_all[:, par, hh, t0:512], False, False, [r_su, r_L[par]], [rb[4 + hh]], skip=True)

            def As(g):
                k = g % nst
                par, t0 = g % 2, t0_of(k)
                TT_("dve", A_one[:, :, t0:512], e_bufs[g % 3][:, :, t0:512], X_bufs[g % 2][:, :, t0:512], ALU.mult,
                    [r_e[g % 3], r_Xs[g % 2]], [r_A])

            def Vs(g):
                pp, k = divmod(g, nst)
                kb, t0 = kbs[k], t0_of(k)
                if k == 0:
                    for _z in range(4):
                        MM(bk[6][:, _z * P:(_z + 1) * P], zeros_bf[:], zeros_bf[:], _z == 0, False, [r_zeros], [rb[6]], skip=True)
                for hh in range(2):
                    hd = 2 * pp + hh
                    pb = 64 * hh
                    MM(bk[6][pb:pb + 64, t0:512], svb[:, kb, hd * 64:(hd + 1) * 64], A_one[:, hh, t0:512], False, False,
                       [r_sv[kb // 4], r_A], [rb[6]], skip=True)
                if k == nst - 1:
                    TS("dve", catS[:, pp, :], bk[6][:], pv[:, 12 + pp:13 + pp], None, ALU.mult, None, [rb[6], r_pv], [r_catS[pp]])
                    ACT(AF.Square, sqs[:], bk[6][:], [rb[6]], [r_sqs])
                    MM(bk[6][:], ones_bf[:], sqs[:], True, True, [r_ones, r_sqs], [rb[6]])
                    if pp == 0:
                        CP("dve", acc_sb, bk[6][:], [rb[6]], [r_xn[1]])
                    else:
                        TT_("dve", acc_sb, acc_sb, bk[6][:], ALU.add, [rb[6], r_xn[1]], [r_xn[1]])

            Zs(0)
            Es(0)
            Ls(0)
            if NG > 1:
                Zs(1)
            for g in range(NG):
                Ts(g)
                if g >= 1:
                    Vs(g - 1)
                if g + 1 < NG:
                    Es(g + 1)
                if g + 2 < NG:
                    Zs(g + 2)
                hg_advance(per_slot)
                Xs(g)
                if g + 1 < NG:
                    Ls(g + 1)
                Ss(g)
                As(g)
            Vs(NG - 1)
            hg_advance(10 ** 9)
            assert hg_state["done"]
            wrel(2)
            TAP('cat_hg_raw', catH[:], r_catH, ti)
            bc_rstd(acc_hg, r_xn[0], HGW, catH, r_catH)
            TAP('cat_hg', catH[:], r_catH, ti)
            TAP('cat_sb_raw', catS[:], r_catS, ti)
            bc_rstd(acc_sb, r_xn[1], SBW, catS, r_catS)
            TAP('cat_sb', catS[:], r_catS, ti)
            fence()
            W0, rW0 = wnext("out0")
            W1, rW1 = wnext("out1")
            for j in range(4):
                b0, b1 = (1, 2) if j % 2 == 0 else (3, 4)
                for half, (bb, W, rW) in enumerate(((b0, W0, rW0), (b1, W1, rW1))):
                    for k in range(8):
                        src, rsrc = (catH[:, k, j * P:(j + 1) * P], r_catH[k]) if k < 4 else (catS[:, k - 4, j * P:(j + 1) * P], r_catS[k - 4])
                        MM(bk[bb][:], src, W[:, k, :], k == 0, k == 7, [rW, rsrc], [rb[bb]])
                    ACT(AF.Square, junk[:, 0:512], bk[bb][:], [rb[bb]], [r_junk, r_ssq], accum_out=ssq[:, 4 + half:5 + half])
                TT_("dve", ssq[:, 6:7], ssq[:, 4:5], ssq[:, 5:6], ALU.add, [r_ssq], [r_ssq])
                ACT(AF.Ln, rstd[:, 4:5], ssq[:, 6:7], [r_ssq], [r_rstd], scale=1.0 / D, bias=EPS)
                ACT(AF.Exp, rstd[:, 4:5], rstd[:, 4:5], [r_rstd], [r_rstd], scale=-0.5)
                for half, bb in enumerate((b0, b1)):
                    tb_, rtb_ = (tmp, r_tmp) if half == 0 else (lnr[:], r_lnr)
                    STT("dve", tb_, bk[bb][:], rstd[:, 4:5], wbc_post[:, half * 512:(half + 1) * 512], ALU.mult, ALU.mult,
                        [rb[bb], r_rstd, r_wbc_post], [rtb_])
                    TT_("pool", xt[:, j, half * 512:(half + 1) * 512], xt[:, j, half * 512:(half + 1) * 512], tb_, ALU.add,
                        [rtb_, r_xt[j]], [r_xt[j]])
            wrel(2)
            TAP('h1', xt[:], r_xt, ti)
            norm_transpose(wbc_fpre, r_wbc_fpre)
            fence()
            gi = 0
            for g, (f0, nf) in enumerate(GROUPS):
                Wg, rWg = wnext("g%d" % g)
                Wu, rWu = wnext("u%d" % g)
                for c in range(nf):
                    bg, bu = (1, 2) if gi % 2 == 0 else (3, 4)
                    gi += 1
                    fm_mm(bg, Wg, rWg, c * P)
                    fm_mm(bu, Wu, rWu, c * P)
                    ACT(AF.Silu, sg, bk[bg][:], [rb[bg]], [r_sg])
                    TT_("dve", actT[:, f0 + c, :], bk[bu][:], sg, ALU.mult, [rb[bu], r_sg], [r_actT])
                wrel(2)
            for fg, (f0, nf) in enumerate(DGRP):
                W, rW = wnext("d0_%d" % fg)
                for j in range(4):
                    for c in range(nf):
                        fc = f0 + c
                        MM(bk[1 + j][:], actT[:, fc, j * P:(j + 1) * P], W[:, c, :], fc == 0, fc == 21, [rW, r_actT], [rb[1 + j]])
                wrel()
            for j in range(4):
                ACT(AF.Square, junk[:, 0:512], bk[1 + j][:], [rb[1 + j]], [r_junk, r_small], accum_out=small[:, 24 + j:25 + j])
                CP("dve", ybuf[:, j, :], bk[1 + j][:], [rb[1 + j]], [r_ybuf])
            Wd = [wnext("d1_%d" % fg) for fg in range(3)]
            for j in range(4):
                for fg, (f0, nf) in enumerate(DGRP):
                    W, rW = Wd[fg]
                    for c in range(nf):
                        fc = f0 + c
                        MM(bk[1 + j][:], actT[:, fc, j * P:(j + 1) * P], W[:, c, :], fc == 0, fc == 21, [rW, r_actT], [rb[1 + j]])
                ACT(AF.Square, junk[:, 0:512], bk[1 + j][:], [rb[1 + j]], [r_junk, r_small], accum_out=small[:, 28 + j:29 + j])
                TT_("dve", ssq[:, j:j + 1], small[:, 24 + j:25 + j], small[:, 28 + j:29 + j], ALU.add, [r_small], [r_ssq])
                ACT(AF.Ln, rstd[:, j:j + 1], ssq[:, j:j + 1], [r_ssq], [r_rstd], scale=1.0 / D, bias=EPS)
                ACT(AF.Exp, rstd[:, j:j + 1], rstd[:, j:j + 1], [r_rstd], [r_rstd], scale=-0.5)
                STT("dve", tmp, ybuf[:, j, :], rstd[:, j:j + 1], wbc_fpost[:, 0:512], ALU.mult, ALU.mult,
                    [r_ybuf, r_rstd, r_wbc_fpost], [r_tmp])
                TT_("pool", xt[:, j, 0:512], xt[:, j, 0:512], tmp, ALU.add, [r_tmp, r_xt[j]], [r_xt[j]])
                STT("dve", lnr[:], bk[1 + j][:], rstd[:, j:j + 1], wbc_fpost[:, 512:1024], ALU.mult, ALU.mult,
                    [rb[1 + j], r_rstd, r_wbc_fpost], [r_lnr])
                TT_("pool", xt[:, j, 512:1024], xt[:, j, 512:1024], lnr[:], ALU.add, [r_lnr, r_xt[j]], [r_xt[j]])
            wrel(3)
            TAP('h2', xt[:], r_xt, ti)
            TAP('actT', actT[:], [r_actT], ti)
            norm_transpose(None, None, src_is_norm=False)
            for j in range(4):
                pg.dma("sp", pt[:], p_in[tok0 + j * P:tok0 + (j + 1) * P, :], writes=[r_pt])
                CP("dve", pn[:], pt[:], [r_pt], [r_pn])
                for k in range(2):
                    pg.op("pe", lambda e, k=k: e.transpose(out=bkT[:, k * P:(k + 1) * P], in_=pn[:, k * P:(k + 1) * P],
                                                            identity=ident[:]), reads=[r_pn, r_ident], writes=[r_bkT])
                CP("act", pT[:, :, j * P:(j + 1) * P], bkT[:, 0:256].rearrange("p (a b) -> p a b", b=P), [r_bkT], [r_pT])
            Wg0, rWg0 = wnext("pg0")
            Wg1, rWg1 = wnext("pg1")
            for j in range(4):
                for half, (W, rW) in enumerate(((Wg0, rWg0), (Wg1, rWg1))):
                    bg, bp = (1, 2) if (2 * j + half) % 2 == 0 else (3, 4)
                    tm_mm(bg, j, W, rW, 0)
                    for k in range(2):
                        MM(bk[bp][:], pT[:, k, j * P:(j + 1) * P], pproj[:, k, half * 512:(half + 1) * 512], k == 0, k == 1,
                           [r_pT, r_pproj], [rb[bp]])
                    ACT(AF.Sigmoid, sg, bk[bg][:], [rb[bg]], [r_sg])
                    tb_, rtb_ = (tmp, r_tmp) if half == 0 else (lnr[:], r_lnr)
                    TT_("dve", tb_, bk[bp][:], sg, ALU.mult, [rb[bp], r_sg], [rtb_])
                    TT_("pool", tb_, tb_, xt[:, j, half * 512:(half + 1) * 512], ALU.add, [rtb_, r_xt[j]], [rtb_])
                    pg.dma("sp", y[tok0 + j * P:tok0 + (j + 1) * P, half * 512:(half + 1) * 512], tb_, reads=[rtb_])
                if ti + 1 < NT:
                    pg.dma("sp", xt[:, j, :], x[tok0 + TT + j * P:tok0 + TT + (j + 1) * P, :], writes=[r_xt[j]])
            wrel(2)
        pg.finish("sp", r_xt + [r_tmp, r_lnr] + tap_outs)
        if _os.environ.get('SBUF_REPORT'):
            print('sbuf remaining', nc.sbuf_bytes_remaining, {e: len(v) for e, v in pg.ops.items()})
        pg.emit()
    return nc


_NC_CACHE = {}


def kernel(**inputs):
    x = np.asarray(inputs["x"], dtype=np.float32)
    B, Tn, _ = x.shape
    if Tn not in _NC_CACHE:
        _NC_CACHE[Tn] = build(Tn)
    nc = _NC_CACHE[Tn]
    shared = {}
    for name in ("attn_pre_norm", "w_in", "hg_out_norm", "sb_out_norm", "w_out", "attn_post_norm", "ffn_pre_norm",
                 "w_gate_up", "w_down", "ffn_post_norm", "ple_proj", "ple_gate"):
        a = np.asarray(inputs[name], dtype=np.float32)
        shared[name] = np.ascontiguousarray(a[0])
    shared["hg_lower_gamma"] = np.ascontiguousarray(np.asarray(inputs["hg_lower_gamma"], dtype=np.float32))
    p = np.asarray(inputs["p"], dtype=np.float32)[0]
    in_maps = []
    for b in range(B):
        m = dict(shared)
        m["x"] = np.ascontiguousarray(x[b])
        m["p"] = np.ascontiguousarray(p[b])
        in_maps.append(m)
    res = run_bass_kernel_spmd(nc, in_maps, core_ids=list(range(B)))
    return np.stack([np.asarray(r["y"]) for r in res.results], axis=0).astype(np.float32)
```
